# Optimizing a Trainium2 kernel written in Bass

```python
import jax, jax.numpy as jnp
from jax import lax
import numpy as np

D_MODEL = 1024
BATCH = 4
SEQ = 4096
DEPTH = 2

N_A_LAYERS = DEPTH // 2
N_B_LAYERS = DEPTH - N_A_LAYERS

D_FF = 2816
RMS_EPS = 1e-6

RWKV_HEAD = 64
RWKV_HEADS = D_MODEL // RWKV_HEAD
DECAY_LORA = 64
AAA_LORA = 64
GATE_LORA = 128
GN_EPS = 64e-5

ATT_HEAD = 64
Q_HEADS = D_MODEL // ATT_HEAD
KV_HEADS = 4
GROUP = Q_HEADS // KV_HEADS
WINDOW = 128
BLOCK = 128
KV_WIDTH = KV_HEADS * ATT_HEAD
MASK_VALUE = -1e30

kernel_name = "yoco_rwkv7_swa_sink_macaron"


def rms_norm(x, g):
    xf = x.astype(jnp.float32)
    y = xf * lax.rsqrt(jnp.mean(xf * xf, axis=-1, keepdims=True) + RMS_EPS)
    return (y * g.astype(jnp.float32)).astype(x.dtype)


def swiglu(x, w_in, w_out):
    gate, up = jnp.split(x @ w_in, 2, axis=-1)
    return (jax.nn.silu(gate) * up) @ w_out


def rwkv7_time_mix(x, mu, w_rkv, w_o, w0, w1, w2, a0, a1, a2, g1, g2,
                   k_k, k_a, r_k, gn_g, gn_b):
    B, T, C = x.shape
    H, N = RWKV_HEADS, RWKV_HEAD
    f32 = jnp.float32
    x_prev = jnp.pad(x, ((0, 0), (1, 0), (0, 0)))[:, :-1]
    xx = x_prev - x
    xr, xw, xk, xv, xa, xg = [x + xx * mu[i] for i in range(6)]

    r = xr @ w_rkv[0]
    k = xk @ w_rkv[1]
    v = xv @ w_rkv[2]
    w = -jax.nn.softplus(-(w0 + jnp.tanh(xw @ w1) @ w2)) - 0.5
    decay = jnp.exp(-jnp.exp(w.astype(f32)))
    a = jax.nn.sigmoid(a0 + (xa @ a1) @ a2)
    g = jax.nn.sigmoid(xg @ g1) @ g2

    heads = lambda t: t.reshape(B, T, H, N).astype(f32)
    kk = heads(k * k_k)
    kk = kk / jnp.maximum(jnp.linalg.norm(kk, axis=-1, keepdims=True), 1e-12)
    k = k * (1.0 + (a - 1.0) * k_a)
    r_h, k_h, v_h, a_h, w_h = heads(r), heads(k), heads(v), heads(a), heads(decay)
    b_h = kk * a_h

    def step(S, inp):
        r_t, w_t, k_t, v_t, kk_t, b_t = inp
        sa = jnp.einsum('bhij,bhj->bhi', S, -kk_t)
        S = (S * w_t[:, :, None, :] + sa[..., None] * b_t[:, :, None, :]
             + v_t[..., None] * k_t[:, :, None, :])
        y = jnp.einsum('bhij,bhj->bhi', S, r_t)
        return S, y

    xs = tuple(jnp.moveaxis(t, 1, 0) for t in (r_h, w_h, k_h, v_h, kk, b_h))
    S0 = jnp.zeros((B, H, N, N), f32)
    _, y = lax.scan(step, S0, xs)
    y = jnp.moveaxis(y, 0, 1)

    mean = jnp.mean(y, axis=-1, keepdims=True)
    var = jnp.mean(jnp.square(y - mean), axis=-1, keepdims=True)
    y = ((y - mean) * lax.rsqrt(var + GN_EPS)).reshape(B, T, C)
    y = y * gn_g.astype(f32) + gn_b.astype(f32)
    bonus = jnp.sum(r_h * k_h * r_k.astype(f32), axis=-1, keepdims=True) * v_h
    y = (y + bonus.reshape(B, T, C)).astype(x.dtype)
    return (y * g) @ w_o


def swa_sinks(xq, k_sh, v_sh, w_q, b_q, w_o, b_o, sinks):
    B, T, _ = xq.shape
    NB = T // BLOCK
    q = (xq @ w_q + b_q).reshape(B, NB, BLOCK, KV_HEADS, GROUP, ATT_HEAD)

    def banded(t):
        t = t.reshape(B, NB, BLOCK, KV_HEADS, ATT_HEAD)
        prev = jnp.pad(t, ((0, 0), (1, 0), (0, 0), (0, 0), (0, 0)))[:, :-1]
        return jnp.concatenate([prev, t], axis=2)

    kb, vb = banded(k_sh), banded(v_sh)
    s = jnp.einsum('bnqkgd,bnskd->bnkgqs', q, kb).astype(jnp.float32) * (ATT_HEAD ** -0.5)
    blk = jnp.arange(NB)[:, None, None] * BLOCK
    q_pos = blk + jnp.arange(BLOCK)[None, :, None]
    k_pos = blk - BLOCK + jnp.arange(2 * BLOCK)[None, None, :]
    valid = (k_pos >= 0) & (k_pos <= q_pos) & (q_pos - k_pos < WINDOW)
    s = jnp.where(valid[None, :, None, None], s, MASK_VALUE)
    sink = jnp.broadcast_to(
        sinks.astype(jnp.float32).reshape(1, 1, KV_HEADS, GROUP, 1, 1), s.shape[:-1] + (1,))
    p = jax.nn.softmax(jnp.concatenate([s, sink], axis=-1), axis=-1)[..., :-1]
    o = jnp.einsum('bnkgqs,bnskd->bnqkgd', p.astype(vb.dtype), vb)
    o = o.reshape(B, T, Q_HEADS * ATT_HEAD)
    return o @ w_o + b_o


def setup_inputs(seed: int = 0) -> dict:
    key = jax.random.key(seed)
    ks = iter(jax.random.split(key, 32))
    D, NA, NBL = D_MODEL, N_A_LAYERS, N_B_LAYERS
    f32 = jnp.float32

    def nrm(shape, scale):
        return scale * jax.random.normal(next(ks), shape, f32)

    def uni(shape, lo, hi):
        return jax.random.uniform(next(ks), shape, f32, lo, hi)

    return {
        "x": nrm((BATCH, SEQ, D), 1.0),
        "norm_g": 1.0 + nrm((DEPTH, 6, D), 0.05),
        "ffn_w_in": nrm((DEPTH, 2, D, 2 * D_FF), D ** -0.5),
        "ffn_w_out": nrm((DEPTH, 2, D_FF, D), D_FF ** -0.5),
        "rwkv_mu": uni((NA, 6, D), 0.0, 1.0),
        "rwkv_w_rkv": nrm((NA, 3, D, D), D ** -0.5),
        "rwkv_w_o": nrm((NA, D, D), D ** -0.5),
        "rwkv_w0": uni((NA, D), -6.0, -1.0),
        "rwkv_w1": nrm((NA, D, DECAY_LORA), D ** -0.5),
        "rwkv_w2": nrm((NA, DECAY_LORA, D), 0.1 * DECAY_LORA ** -0.5),
        "rwkv_a0": nrm((NA, D), 0.1),
        "rwkv_a1": nrm((NA, D, AAA_LORA), D ** -0.5),
        "rwkv_a2": nrm((NA, AAA_LORA, D), 0.1 * AAA_LORA ** -0.5),
        "rwkv_g1": nrm((NA, D, GATE_LORA), D ** -0.5),
        "rwkv_g2": nrm((NA, GATE_LORA, D), GATE_LORA ** -0.5),
        "rwkv_k_k": 0.85 + nrm((NA, D), 0.05),
        "rwkv_k_a": 1.0 + nrm((NA, D), 0.05),
        "rwkv_r_k": nrm((NA, RWKV_HEADS, RWKV_HEAD), 0.1),
        "rwkv_gn_g": 1.0 + nrm((NA, D), 0.05),
        "rwkv_gn_b": nrm((NA, D), 0.02),
        "kv_norm_g": 1.0 + nrm((D,), 0.05),
        "w_kv": nrm((D, 2 * KV_WIDTH), D ** -0.5),
        "b_kv": nrm((2 * KV_WIDTH,), 0.02),
        "attn_w_q": nrm((NBL, D, Q_HEADS * ATT_HEAD), D ** -0.5),
        "attn_b_q": nrm((NBL, Q_HEADS * ATT_HEAD), 0.02),
        "attn_w_o": nrm((NBL, Q_HEADS * ATT_HEAD, D), (Q_HEADS * ATT_HEAD) ** -0.5),
        "attn_b_o": nrm((NBL, D), 0.02),
        "attn_sinks": nrm((NBL, Q_HEADS), 1.0),
    }


def reference(x, norm_g, ffn_w_in, ffn_w_out,
              rwkv_mu, rwkv_w_rkv, rwkv_w_o, rwkv_w0, rwkv_w1, rwkv_w2,
              rwkv_a0, rwkv_a1, rwkv_a2, rwkv_g1, rwkv_g2, rwkv_k_k, rwkv_k_a,
              rwkv_r_k, rwkv_gn_g, rwkv_gn_b,
              kv_norm_g, w_kv, b_kv,
              attn_w_q, attn_b_q, attn_w_o, attn_b_o, attn_sinks):
    B, T, _ = x.shape
    h = x
    k_sh = v_sh = None
    for layer in range(DEPTH):
        g = norm_g[layer]
        h = h + 0.5 * rms_norm(swiglu(rms_norm(h, g[0]), ffn_w_in[layer, 0], ffn_w_out[layer, 0]), g[1])
        u = rms_norm(h, g[2])
        if layer < N_A_LAYERS:
            i = layer
            m = rwkv7_time_mix(u, rwkv_mu[i], rwkv_w_rkv[i], rwkv_w_o[i], rwkv_w0[i],
                               rwkv_w1[i], rwkv_w2[i], rwkv_a0[i], rwkv_a1[i], rwkv_a2[i],
                               rwkv_g1[i], rwkv_g2[i], rwkv_k_k[i], rwkv_k_a[i],
                               rwkv_r_k[i], rwkv_gn_g[i], rwkv_gn_b[i])
        else:
            j = layer - N_A_LAYERS
            m = swa_sinks(u, k_sh, v_sh, attn_w_q[j], attn_b_q[j], attn_w_o[j],
                          attn_b_o[j], attn_sinks[j])
        h = h + rms_norm(m, g[3])
        h = h + 0.5 * rms_norm(swiglu(rms_norm(h, g[4]), ffn_w_in[layer, 1], ffn_w_out[layer, 1]), g[5])
        if layer == N_A_LAYERS - 1:
            kv = rms_norm(h, kv_norm_g) @ w_kv + b_kv
            k_sh = kv[..., :KV_WIDTH].reshape(B, T, KV_HEADS, ATT_HEAD)
            v_sh = kv[..., KV_WIDTH:].reshape(B, T, KV_HEADS, ATT_HEAD)
    return h
```

```python
import numpy as np
from contextlib import ExitStack
import concourse.bass as bass
import concourse.mybir as mybir
from concourse.bass_utils import run_bass_kernel_spmd

F32 = mybir.dt.float32
BF16 = mybir.dt.bfloat16
AF = mybir.ActivationFunctionType
ALU = mybir.AluOpType
AX = mybir.AxisListType

D = 1024
DFF = 2816
NFC = 22
RMS_EPS = 1e-6
GN_EPS = 64e-5
SAME_ENGINE_SYNC = True


class R:
    __slots__ = ("ap", "keys")

    def __init__(self, ap, *keys):
        self.ap = ap
        self.keys = tuple(keys)

    def __getitem__(self, sl):
        return R(self.ap[sl], *self.keys)


class Op:
    __slots__ = ("eng", "fn", "deps", "signal", "semval", "dma_sem", "idx", "is_dma")

    def __init__(self, eng, fn, is_dma, dma_sem):
        self.eng = eng
        self.fn = fn
        self.deps = {}
        self.signal = False
        self.semval = 0
        self.dma_sem = dma_sem
        self.is_dma = is_dma


class Prog:
    COMPUTE = ("pe", "act", "dve", "pool")

    def __init__(self, nc):
        self.nc = nc
        self.ops = []
        self.last_w = {}
        self.readers = {}

    def add(self, eng, fn, reads=(), writes=(), dma_sem=None):
        op = Op(eng, fn, dma_sem is not None, dma_sem)
        op.idx = len(self.ops)
        deps = {}
        for k in reads:
            w = self.last_w.get(k)
            if w is not None:
                deps[w] = "w"
        for k in writes:
            w = self.last_w.get(k)
            if w is not None:
                deps[w] = "w"
            for r in self.readers.get(k, ()):
                if r not in deps:
                    deps[r] = "r"
        for k in reads:
            if isinstance(k, tuple) and k and k[0] == "ps":
                for r in self.readers.get(k, ()):
                    if r not in deps:
                        deps[r] = "x"
        deps.pop(op, None)
        for k in reads:
            self.readers.setdefault(k, []).append(op)
        for k in writes:
            self.last_w[k] = op
            self.readers[k] = []
        for d, kind in deps.items():
            if d.is_dma or op.is_dma:
                need = True
                if d.is_dma and op.is_dma and False:
                    need = True
            elif d.eng == op.eng:
                if op.eng == "pe":
                    need = False
                else:
                    need = SAME_ENGINE_SYNC and kind == "w"
            else:
                need = True
            if need:
                op.deps[d] = kind
                d.signal = True
        self.ops.append(op)
        return op

    def emit(self, es):
        nc = self.nc
        sems = {}
        LIM = 16000
        counts = {e: 0 for e in self.COMPUTE}
        dcounts = {}
        for op in self.ops:
            if op.is_dma:
                if op.dma_sem not in sems:
                    sems[op.dma_sem] = es.enter_context(nc.semaphore("dsem_" + op.dma_sem))
                    dcounts[op.dma_sem] = 0
                dcounts[op.dma_sem] += 16
                op.semval = dcounts[op.dma_sem]
            elif op.signal:
                counts[op.eng] += 1
                ep = (counts[op.eng] - 1) // LIM
                op.dma_sem = f"{op.eng}{ep}"
                if op.dma_sem not in sems:
                    sems[op.dma_sem] = es.enter_context(nc.semaphore("sem_" + op.dma_sem))
                op.semval = counts[op.eng] - ep * LIM
        self.stats = dict(counts=counts, dcounts=dcounts, nops=len(self.ops))
        by_eng = {}
        for op in self.ops:
            by_eng.setdefault(op.eng, []).append(op)
        block = es.enter_context(nc.Block())
        final_waits = [(sems[s], v) for s, v in dcounts.items() if s.startswith("out")]

        def run(eng_name, e):
            waited = {}
            for op in by_eng.get(eng_name, []):
                need = {}
                for d in op.deps:
                    s = d.dma_sem
                    if d.semval > need.get(s, 0):
                        need[s] = d.semval
                for s, v in need.items():
                    if waited.get(s, 0) >= v:
                        continue
                    e.wait_ge(sems[s], v)
                    waited[s] = v
                inst = op.fn(e)
                if op.is_dma:
                    inst.then_inc(sems[op.dma_sem], 16)
                elif op.signal:
                    inst.then_inc(sems[op.dma_sem], 1)
            if eng_name == "sp":
                for s, v in final_waits:
                    e.wait_ge(s, v)

        @block.tensor
        def _(e):
            run("pe", e)

        @block.scalar
        def _(e):
            run("act", e)

        @block.vector
        def _(e):
            run("dve", e)

        @block.gpsimd
        def _(e):
            run("pool", e)

        @block.sync
        def _(e):
            run("sp", e)

    @staticmethod
    def _keys(*rs):
        ks = []
        for r in rs:
            if isinstance(r, R):
                ks.extend(r.keys)
        return ks

    @staticmethod
    def _ap(x):
        return x.ap if isinstance(x, R) else x

    def mm(self, out, lhsT, rhs, start=True, stop=True):
        o, l, r = out.ap, lhsT.ap, rhs.ap
        return self.add("pe", lambda e: e.matmul(o, l, r, start=start, stop=stop),
                        reads=self._keys(lhsT, rhs), writes=out.keys)

    def tr(self, out, in_, ident):
        o, i, d = out.ap, in_.ap, ident.ap
        return self.add("pe", lambda e: e.transpose(o, i, d),
                        reads=self._keys(in_, ident), writes=out.keys)

    def act(self, out, in_, func, bias=None, scale=1.0, accum=None, eng="act"):
        o, i = out.ap, in_.ap
        b = self._ap(bias) if bias is not None else None
        sc = self._ap(scale)
        ac = accum.ap if accum is not None else None
        kw = {}
        if b is not None:
            kw["bias"] = b
        if ac is not None:
            kw["accum_out"] = ac
        return self.add(eng, lambda e: e.activation(out=o, in_=i, func=func, scale=sc, **kw),
                        reads=self._keys(in_, bias, scale),
                        writes=list(out.keys) + (list(accum.keys) if accum is not None else []))

    def tt(self, eng, out, in0, in1, op):
        o, a, b = out.ap, in0.ap, in1.ap
        return self.add(eng, lambda e: e.tensor_tensor(o, a, b, op),
                        reads=self._keys(in0, in1), writes=out.keys)

    def ts(self, eng, out, in0, s1, s2, op0, op1=None):
        o, a = out.ap, in0.ap
        x1, x2 = self._ap(s1), self._ap(s2)
        if eng == "act_mul":
            return self.add("act", lambda e: e.mul(o, a, x1), reads=self._keys(in0, s1), writes=out.keys)
        eng = "dve"
        if op1 is None:
            fn = lambda e: e.tensor_scalar(o, a, x1, None, op0)
        else:
            fn = lambda e: e.tensor_scalar(o, a, x1, x2, op0, op1)
        return self.add(eng, fn, reads=self._keys(in0, s1, s2), writes=out.keys)

    def stt(self, eng, out, in0, scalar, in1, op0, op1):
        o, a, b = out.ap, in0.ap, in1.ap
        s = self._ap(scalar)
        eng = "dve"
        return self.add(eng, lambda e: e.scalar_tensor_tensor(o, a, s, b, op0, op1),
                        reads=self._keys(in0, scalar, in1), writes=out.keys)

    def copy(self, eng, out, in_):
        o, i = out.ap, in_.ap
        if eng == "act":
            return self.add(eng, lambda e: e.copy(o, i), reads=in_.keys, writes=out.keys)
        return self.add(eng, lambda e: e.tensor_copy(o, i), reads=in_.keys, writes=out.keys)

    def memset(self, eng, out, val):
        o = out.ap
        return self.add(eng, lambda e: e.memset(o, val), reads=(), writes=out.keys)

    def recip(self, out, in_):
        o, i = out.ap, in_.ap
        return self.add("dve", lambda e: e.reciprocal(o, i), reads=in_.keys, writes=out.keys)

    def reduce(self, eng, out, in_, op, axis=AX.X):
        o, i = out.ap, in_.ap
        return self.add(eng, lambda e: e.tensor_reduce(o, i, axis, op), reads=in_.keys, writes=out.keys)

    def scan(self, out, d0, d1, init, op0, op1):
        o, a, b = out.ap, d0.ap, d1.ap
        return self.add("dve", lambda e: e.tensor_tensor_scan(o, a, b, init, op0, op1),
                        reads=self._keys(d0, d1), writes=out.keys)

    def dma(self, eng, out, in_, sem):
        o, i = self._ap(out), self._ap(in_)
        if sem == "par":
            self._npar = getattr(self, "_npar", 0) + 1
            sem = f"par{self._npar}"
        return self.add(eng, lambda e: e.dma_start(out=o, in_=i),
                        reads=self._keys(in_), writes=self._keys(out), dma_sem=sem)

    def barrier(self):
        last = {}
        dmas = []
        for op in self.ops[self._bar_from:]:
            if op.is_dma:
                dmas.append(op)
            else:
                last[op.eng] = op
        self._bar_from = len(self.ops)
        deps = list(last.values()) + dmas
        prev = getattr(self, "_pending", {})
        self._pending = {e: list(deps) + prev.get(e, []) for e in ("pe", "act", "dve", "pool", "sp")}


_orig_add = Prog.add


def _add_with_barrier(self, eng, fn, reads=(), writes=(), dma_sem=None):
    op = _orig_add(self, eng, fn, reads, writes, dma_sem)
    pend = getattr(self, "_pending", None)
    if pend and eng in pend:
        for d in pend.pop(eng):
            if d is op:
                continue
            if (not d.is_dma) and d.eng == eng and eng == "pe":
                continue
            op.deps[d] = "w"
            d.signal = True
    return op


Prog.add = _add_with_barrier
Prog._bar_from = 0


class Alloc:
    def __init__(self, nc, base=16512, limit=229344):
        self.nc = nc
        self.off = base
        self.limit = limit
        self.n = 0

    def mark(self):
        return self.off

    def reset(self, off):
        self.off = off

    def t(self, name, shape, dtype):
        size = int(np.prod(shape[1:])) * (4 if dtype == F32 else 2)
        self.off = (self.off + 63) // 64 * 64
        assert self.off + size <= self.limit, (name, self.off, size)
        self.n += 1
        h = self.nc.alloc_sbuf_tensor_at(f"{name}_{self.n}", list(shape), dtype, offset=self.off)
        self.off += size
        return h


class K:
    pass


def ps_bank(k, b, n=512):
    t = k.ps[b // 2]
    off = (b % 2) * 512
    return R(t[:, off:off + n], ("ps", b))


def ps_pair(k, i):
    return R(k.ps[i][:, :], ("ps", 2 * i), ("ps", 2 * i + 1))


def rstd_from_ss(k, p, ss, rstd, n, scale, bias_tile):
    p.act(rstd[:, 0:n], ss[:, 0:n], AF.Sqrt, bias=bias_tile, scale=scale)
    p.recip(rstd[:, 0:n], rstd[:, 0:n])


def ffn(k, p, srcs, dsts, layer, which, wsem_base):
    G = len(srcs)
    NT = G * 128
    a = k.al
    m0 = a.mark()
    xnT = a.t("xnT", [128, 8, NT], BF16)
    hidT = a.t("hidT", [128, NFC, NT], BF16)
    win = [a.t("win", [128, 8, 2, 256], BF16) for _ in range(2)]
    wout = a.t("wout", [128, NFC, 1024], BF16)
    xn = [a.t("xn", [128, 1024], F32) for _ in range(2)]
    sg = [a.t("sg", [128, 512], F32) for _ in range(2)]
    tmp = [a.t("tmp", [128, 1024], F32) for _ in range(2)]
    gpre = a.t("gpre", [128, 1024], F32)
    gpost = a.t("gpost", [128, 1024], F32)
    a.reset(m0)
    tag = f"f{layer}{which}"
    R_xnT = lambda s: R(xnT[:, :, s * 128:(s + 1) * 128], ("xnT", s))
    gi = 0 if which == 0 else 4
    R_gpre = R(gpre[:, :], "gpre")
    R_gpost = R(gpost[:, :], "gpost")
    p.dma("sp", R_gpre, k.gbc[layer * 6 + gi], "par")
    p.dma("sp", R_gpost, k.gbc[layer * 6 + gi + 1], "par")
    ss = R(k.ss[:, :], "ss")
    rstd = R(k.rstd[:, :], "rstd")
    w_in_v = k.ffn_w_in[layer, which].rearrange("(dc p) (two f) -> p dc two f", p=128, two=2)
    w_out_v = k.ffn_w_out[layer, which].rearrange("(fc p) d -> p fc d", p=128)
    NFG = 11
    R_win = lambda s: R(win[s][:, :, :, :], ("win", s))
    win_issued = [0]

    def issue_win(fg):
        s = fg % 2
        for two in range(2):
            p.dma("pool", R(win[s][:, :, two, :], ("win", s, two)),
                  w_in_v[:, :, two, fg * 256:(fg + 1) * 256], f"win{s}{two}")

    issue_win(0)
    issue_win(1)
    for s in range(G):
        junk = R(tmp[s % 2][:, :], ("tmp", s % 2))
        p.act(junk, srcs[s], AF.Square, accum=ss[:, s:s + 1])
    rstd_from_ss(k, p, ss, rstd, G, 1.0 / D, R(k.eps1[:, :], "const"))
    for s in range(G):
        xs = R(xn[s % 2][:, :], ("xn", s % 2))
        p.stt("dve", xs, srcs[s], rstd[:, s:s + 1], R_gpre, ALU.mult, ALU.mult)
        pt = ps_pair(k, 2 + (s % 2))
        for c in range(8):
            p.tr(pt[:, c * 128:(c + 1) * 128], xs[:, c * 128:(c + 1) * 128], R(k.identF[:, :], "const"))
        eng = "act" if s % 2 == 0 else "dve"
        p.copy(eng, R_xnT(s), R(pt.ap.rearrange("p (c t) -> p c t", c=8), *pt.keys))
    R_wout = lambda hlf: R(wout[:, hlf * 11:(hlf + 1) * 11, :], ("wout", hlf))
    for hlf in range(2):
        p.dma("pool", R_wout(hlf), w_out_v[:, hlf * 11:(hlf + 1) * 11, :], f"wout{hlf}")
    nblocks = [(i, min(i + 512, NT)) for i in range(0, NT, 512)]
    it = 0
    for fg in range(NFG):
        s_w = fg % 2
        for jj in range(2):
            j = fg * 2 + jj
            for (n0, n1) in nblocks:
                nn = n1 - n0
                set_ = it % 2
                it += 1
                pg = ps_bank(k, set_ * 2, nn)
                pu = ps_bank(k, set_ * 2 + 1, nn)
                rhs_keys = [("xnT", s) for s in range(n0 // 128, n1 // 128)]
                for two, pt in ((0, pg), (1, pu)):
                    for dc in range(8):
                        p.mm(pt, R(win[s_w][:, dc, two, jj * 128:(jj + 1) * 128], ("win", s_w, two)),
                             R(xnT[:, dc, n0:n1], *rhs_keys), start=(dc == 0), stop=(dc == 7))
                sgt = R(sg[set_][:, 0:nn], ("sg", set_))
                p.act(sgt, pg, AF.Silu)
                p.tt("dve", R(hidT[:, j, n0:n1], ("hid", j)), sgt, pu, ALU.mult)
        if fg + 2 < NFG:
            issue_win(fg + 2)
    for s in range(G):
        po = ps_pair(k, s % 2)
        for dh in range(2):
            for j in range(NFC):
                p.mm(po[:, dh * 512:(dh + 1) * 512], R(hidT[:, j, s * 128:(s + 1) * 128], ("hid", j)),
                     R(wout[:, j, dh * 512:(dh + 1) * 512], ("wout", j // 11)),
                     start=(j == 0), stop=(j == NFC - 1))
        ss2 = R(k.ss2[:, :], ("ss2", s % 2))
        rs2 = R(k.rstd2[:, :], ("rs2", s % 2))
        junk = R(tmp[s % 2][:, :], ("tmp", s % 2))
        p.act(junk, po, AF.Square, accum=ss2[:, s % 2:s % 2 + 1])
        p.act(rs2[:, s % 2:s % 2 + 1], ss2[:, s % 2:s % 2 + 1], AF.Sqrt, bias=R(k.eps4[:, :], "const"), scale=4.0 / D)
        p.recip(rs2[:, s % 2:s % 2 + 1], rs2[:, s % 2:s % 2 + 1])
        p.stt("dve", junk, po, rs2[:, s % 2:s % 2 + 1], R_gpost, ALU.mult, ALU.mult)
        p.tt("pool", dsts[s], srcs[s], junk, ALU.add)


def declare_inputs(nc, NTW, NOUT):
    di = {}

    def inp(name, shape):
        di[name] = nc.dram_tensor(name, list(shape), F32, kind="ExternalInput").ap()

    inp("xw", [NTW, D])
    inp("gbc", [13, 128, D])
    inp("ffn_w_in", [2, 2, D, 2 * DFF])
    inp("ffn_w_out", [2, 2, DFF, D])
    inp("w_rkv", [3, D, D])
    inp("w_ro", [D, D])
    inp("w1", [D, 64])
    inp("w2", [64, D])
    inp("a1", [D, 64])
    inp("a2", [64, D])
    inp("g1", [D, 128])
    inp("g2", [128, D])
    inp("cm", [128, 16, 8])
    inp("tmb", [4, 128, D])
    inp("w_kv", [D, 512])
    inp("w_q", [D, D])
    inp("w_ao", [D, D])
    inp("consts", [6, 128, 512])
    inp("mfirst", [128, 128])
    di["out"] = nc.dram_tensor("out", [NOUT, D], F32, kind="ExternalOutput").ap()
    return di


def build(cfg):
    nc = bass.Bass("TRN2", target_bir_lowering=False)
    chunks = cfg["chunks"]
    NTW = cfg["ntw"]
    NOUT = cfg["nout"]
    di = declare_inputs(nc, NTW, NOUT)
    k = K()
    k.nc = nc
    k.di = di
    k.gbc = [R(di["gbc"][i], ) for i in range(13)]
    k.ffn_w_in = di["ffn_w_in"]
    k.ffn_w_out = di["ffn_w_out"]
    es = ExitStack()
    k.ps = [es.enter_context(nc.psum_tensor(f"ps{i}", [128, 1024], F32)) for i in range(4)]
    al = Alloc(nc)
    k.al = al
    p = Prog(nc)
    k.p = p
    k.h = al.t("h", [128, 8, D], F32)
    k.identF = al.t("identF", [128, 128], F32)
    k.cst = al.t("cst", [128, 5, 512], BF16)
    k.eps1 = al.t("eps1", [128, 1], F32)
    k.eps4 = al.t("eps4", [128, 1], F32)
    k.ss = al.t("ss", [128, 16], F32)
    k.rstd = al.t("rstd", [128, 16], F32)
    k.ss2 = al.t("ss2", [128, 2], F32)
    k.rstd2 = al.t("rstd2", [128, 2], F32)
    _j = al.t("junk", [128, D], F32)
    k.junk = [_j, _j]
    k.ST = al.t("ST", [128, 8, 64], F32)
    k.STb = al.t("STb", [128, 8, 64], BF16)
    k.ulast = al.t("ulast", [128, 8, 1], F32)
    k.eps24 = al.t("eps24", [128, 1], F32)
    k.epsgn = al.t("epsgn", [128, 1], F32)
    k.cm = al.t("cm", [128, 16, 8], F32)
    p.dma("sp", R(k.identF[:, :], "const"), di["consts"][0][:, 0:128], "par")
    p.dma("pool", R(k.cst[:, :, :], "const"), di["consts"][1:6].rearrange("a p n -> p a n"), "parc")
    p.dma("sp", R(k.cm[:, :, :], "const"), di["cm"], "par")
    k.one1 = al.t("one1", [128, 1], F32)
    k.zero1 = al.t("zero1", [128, 1], F32)
    k.mone1 = al.t("mone1", [128, 1], F32)
    k.ncm = al.t("ncm", [128, 2, 8], F32)
    p.memset("dve", R(k.one1[:, :], "const"), 1.0)
    p.memset("dve", R(k.zero1[:, :], "const"), 0.0)
    p.memset("dve", R(k.mone1[:, :], "const"), -1.0)
    k.hm = al.t("hm", [128, 2], F32)
    p.copy("dve", R(k.hm[:, :], "const"), R(k.cst[:, 2, 128:130], "const"))
    k.KTm = al.t("KTm", [128, 2, 1152], BF16)
    k.VA = al.t("VA", [128, 9, 4, 65], BF16)
    p.memset("dve", R(k.KTm[:, :, :], *[("kt", j) for j in range(9)]), 0.0)
    p.memset("dve", R(k.VA[:, :, :, :], *[("va", j) for j in range(9)]), 1.0)
    p.ts("dve", R(k.ncm[:, :, :], "const"), R(k.cm[:, 6:8, :], "const"), -1.0, None, ALU.mult)
    p.memset("dve", R(k.eps24[:, :], "const"), 1e-24)
    p.memset("dve", R(k.epsgn[:, :], "const"), GN_EPS)
    p.memset("dve", R(k.ST[:, :, :], *[("ST", c) for c in range(8)]), 0.0)
    p.memset("dve", R(k.STb[:, :, :], *[("STb", c) for c in range(8)]), 0.0)
    p.memset("dve", R(k.eps1[:, :], "const"), RMS_EPS)
    p.memset("dve", R(k.eps4[:, :], "const"), 4.0 * RMS_EPS)
    hs = lambda j: R(k.h[:, j, :], ("h", j))
    xw = di["xw"].rearrange("(n p) d -> n p d", p=128)
    outv = di["out"].rearrange("(n p) d -> n p d", p=128)
    stages = cfg["stages"]
    for ci, ch in enumerate(chunks):
        ns = ch["nsub"]
        k.ns_kv = ns
        for j in range(ns):
            p.dma("sp", hs(j), xw[ch["sub0"] + j], f"x{j}")
        ffn(k, p, [hs(j) for j in range(ns)], [hs(j) for j in range(ns)], 0, 0, "a")
        p.barrier()
        if stages == "ffn1":
            for j in range(ns):
                p.dma("sp", outv[ch["out0"] + j], hs(j), "out")
            continue
        rwkv_chunk(k, p, ns, hs, set(ch["out_units"]), ci == 0)
        p.barrier()
        if stages == "rwkv":
            for j in range(ns):
                p.dma("sp", outv[ch["out0"] + j], hs(j), "out")
            continue
        kvs = ch["kv_slots"]
        if kvs:
            ffn(k, p, [hs(j) for j in kvs], [hs(j) for j in kvs], 0, 1, "b")
            p.barrier()
            kv_proj(k, p, kvs, hs, ch["kv_shift"])
            p.barrier()
        if not ch["own"]:
            continue
        if stages == "kv":
            for j in range(ns):
                p.dma("sp", outv[ch["out0"] + j], hs(j), "out")
            continue
        ffn(k, p, [hs(j) for j in range(ns)], [hs(j) for j in range(ns)], 1, 0, "c")
        p.barrier()
        attention(k, p, ns, hs, ch["first_own"])
        p.barrier()
        if stages == "attn":
            for j in range(ns):
                p.dma("sp", outv[ch["out0"] + j], hs(j), "out")
            continue
        ffn(k, p, [hs(j) for j in range(ns)], [hs(j) for j in range(ns)], 1, 1, "d")
        p.barrier()
        for j in range(ns):
            p.dma("sp", outv[ch["out0"] + j], hs(j), "out")
    p.emit(es)
    es.close()
    return nc, p


def host_prep(inputs):
    f = np.float32
    g = {}
    ng = np.asarray(inputs["norm_g"], f)
    gb = np.concatenate([ng.reshape(12, D), np.asarray(inputs["kv_norm_g"], f).reshape(1, D)], 0)
    g["gbc"] = np.ascontiguousarray(np.broadcast_to(gb[:, None, :], (13, 128, D)))
    g["ffn_w_in"] = np.asarray(inputs["ffn_w_in"], f)
    g["ffn_w_out"] = np.asarray(inputs["ffn_w_out"], f)
    g["w_rkv"] = np.asarray(inputs["rwkv_w_rkv"], f)[0]
    g["w_ro"] = np.asarray(inputs["rwkv_w_o"], f)[0]
    for nm in ("w1", "w2", "a1", "a2", "g1", "g2"):
        g[nm] = np.asarray(inputs["rwkv_" + nm], f)[0]
    cm = np.zeros((16, D), f)
    cm[0:6] = np.asarray(inputs["rwkv_mu"], f)[0]
    cm[6] = np.asarray(inputs["rwkv_w0"], f)[0]
    cm[7] = np.asarray(inputs["rwkv_a0"], f)[0]
    cm[8] = np.asarray(inputs["rwkv_k_k"], f)[0]
    cm[9] = np.asarray(inputs["rwkv_k_a"], f)[0]
    cm[10] = np.asarray(inputs["rwkv_r_k"], f)[0].reshape(D)
    bq = np.asarray(inputs["attn_b_q"], f)[0].reshape(16, 64)
    perm = []
    for t in range(8):
        pp, i = t // 4, t % 4
        perm += [8 * pp + i, 8 * pp + 4 + i]
    cm[11] = bq[perm].reshape(D)
    bkv = np.asarray(inputs["b_kv"], f)
    cm[12, 0:256] = bkv[0:256]
    cm[13] = ng[0, 2]
    cm[14] = np.asarray(inputs["kv_norm_g"], f)
    cm[15] = ng[1, 2]
    g["cm"] = np.ascontiguousarray(cm.reshape(16, 8, 128).transpose(2, 0, 1))
    tmb = np.zeros((4, D), f)
    tmb[0] = np.asarray(inputs["rwkv_gn_g"], f)[0]
    tmb[1] = np.asarray(inputs["rwkv_gn_b"], f)[0]
    tmb[2] = np.asarray(inputs["attn_b_o"], f)[0]
    tmb[3, 0:256] = bkv[256:512]
    tmb[3, 256:272] = np.asarray(inputs["attn_sinks"], f)[0]
    g["tmb"] = np.ascontiguousarray(np.broadcast_to(tmb[:, None, :], (4, 128, D)))
    g["w_kv"] = np.asarray(inputs["w_kv"], f)
    wqh = np.asarray(inputs["attn_w_q"], f)[0].reshape(D, 16, 64)
    g["w_q"] = np.ascontiguousarray(wqh[:, perm, :].reshape(D, D))
    g["w_ao"] = np.asarray(inputs["attn_w_o"], f)[0]
    cs = np.zeros((6, 128, 512), f)
    cs[0, :, 0:128] = np.eye(128, dtype=f)
    ii = np.arange(128)
    same = (ii[:, None] // 64) == (ii[None, :] // 64)
    MS = ((ii[:, None] < ii[None, :]) & same).astype(f)
    MI = ((ii[:, None] <= ii[None, :]) & same).astype(f)
    cs[1] = np.concatenate([MS, MI, MS, MI], 1)
    cs[2, :, 0:256] = np.concatenate([MS.T, MS.T], 1)
    cs[3, :, 0:128] = same.astype(f)
    cs[3, :, 128] = (ii < 64)
    cs[3, :, 129] = (ii >= 64)
    cs[4, :, 0:128] = (ii[:, None] <= ii[None, :])
    cs[4, :, 128:256] = (ii[:, None] > ii[None, :])
    cs[5, :, 0:64] = 1.0
    g["consts"] = cs
    return g


C_DEC = 0.6065306597126334


def rsqrt_act(p, out, in_, bias, scale):
    p.act(out, in_, AF.Ln, bias=bias, scale=scale)
    p.act(out, out, AF.Exp, scale=-0.5)


def sigmoid_act(p, out, in_, nbias, ones, scale=1.0):
    p.act(out, in_, AF.Exp, bias=nbias, scale=-scale)
    p.act(out, out, AF.Ln, bias=ones, scale=1.0)
    p.act(out, out, AF.Exp, scale=-1.0)


def post_norm_residual(k, p, po, hdst, gR, s, half, junk=None):
    ss2 = R(k.ss2[:, :], ("ss2", s % 2))
    rs2 = R(k.rstd2[:, :], ("rs2", s % 2))
    if junk is None:
        junk = R(k.junk[s % 2][:, :], ("junk", 0))
    c = s % 2
    p.act(junk, po, AF.Square, accum=ss2[:, c:c + 1])
    if half:
        rsqrt_act(p, rs2[:, c:c + 1], ss2[:, c:c + 1], R(k.eps4[:, :], "const"), 4.0 / D)
    else:
        rsqrt_act(p, rs2[:, c:c + 1], ss2[:, c:c + 1], R(k.eps1[:, :], "const"), 1.0 / D)
    p.stt("dve", junk, po, rs2[:, c:c + 1], gR, ALU.mult, ALU.mult)
    p.tt("pool", hdst, hdst, junk, ALU.add)


def norm_transpose(k, p, src, gR, dstT, s, scratch, pair, gcm=None):
    ss = R(k.ss[:, :], "ss")
    rstd = R(k.rstd[:, :], "rstd")
    junk = R(k.junk[s % 2][:, :], ("junk", 0))
    p.act(junk, src, AF.Square, accum=ss[:, s:s + 1])
    rsqrt_act(p, rstd[:, s:s + 1], ss[:, s:s + 1], R(k.eps1[:, :], "const"), 1.0 / D)
    if gR is not None:
        p.stt("dve", scratch, src, rstd[:, s:s + 1], gR, ALU.mult, ALU.mult)
    else:
        p.ts("dve", scratch, src, rstd[:, s:s + 1], None, ALU.mult)
    pt = ps_pair(k, pair)
    for c in range(8):
        p.tr(pt[:, c * 128:(c + 1) * 128], scratch[:, c * 128:(c + 1) * 128], R(k.identF[:, :], "const"))
    if gR is not None:
        p.copy("act", dstT, R(pt.ap.rearrange("p (c t) -> p c t", c=8), *pt.keys))
    else:
        for c in range(8):
            p.ts("dve" if c % 2 else "act_mul", dstT[:, c, :], pt[:, c * 128:(c + 1) * 128],
                 R(k.cm[:, gcm, c:c + 1], "const"), None, ALU.mult)


def rwkv_chunk(k, p, ns, hs, out_units, first_unit_of_core):
    di = k.di
    a = k.al
    m0 = a.mark()
    cst = k.cst
    cm = k.cm
    CK = lambda i, ct: R(cm[:, i, ct:ct + 1], "const")
    wts = {}
    for i, nm in enumerate(("wr", "wk", "wv")):
        wts[nm] = a.t(nm, [128, 8, D], BF16)
        p.dma("pool", R(wts[nm][:, :, :], nm), di["w_rkv"][i].rearrange("(dc p) f -> p dc f", p=128), "rw" + nm)
    wts["wo"] = a.t("wo", [128, 8, D], BF16)
    p.dma("pool", R(wts["wo"][:, :, :], "wo"), di["w_ro"].rearrange("(dc p) f -> p dc f", p=128), "rwwo")
    w1s = a.t("w1s", [128, 8, 64], BF16)
    a1s = a.t("a1s", [128, 8, 64], BF16)
    g1s = a.t("g1s", [128, 8, 128], BF16)
    p.dma("pool", R(w1s[:, :, :], "w1s"), di["w1"].rearrange("(dc p) f -> p dc f", p=128), "rwl1")
    p.dma("pool", R(a1s[:, :, :], "a1s"), di["a1"].rearrange("(dc p) f -> p dc f", p=128), "rwl2")
    p.dma("pool", R(g1s[:, :, :], "g1s"), di["g1"].rearrange("(dc p) f -> p dc f", p=128), "rwl3")
    w2s = a.t("w2s", [64, D], BF16)
    a2s = a.t("a2s", [64, D], BF16)
    g2s = a.t("g2s", [128, D], BF16)
    p.dma("pool", R(w2s[:, :], "w2s"), di["w2"], "rwl4")
    p.dma("pool", R(a2s[:, :], "a2s"), di["a2"], "rwl5")
    p.dma("pool", R(g2s[:, :], "g2s"), di["g2"], "rwl6")
    uT = a.t("uT", [128, 8, 257], BF16)
    m1 = a.mark()
    xx = a.t("xx", [128, 8, 256], BF16)
    xm = [a.t("xm", [128, 8, 256], BF16) for _ in range(2)]
    m2 = a.mark()
    a.reset(m1)
    gng = a.t("gng", [128, D], F32)
    gnb = a.t("gnb", [128, D], F32)
    g3bc = a.t("g3bc", [128, D], F32)
    assert a.mark() == m2
    R_gng = R(gng[:, :], "xx")
    R_gnb = R(gnb[:, :], *[("xm", 0, c) for c in range(8)])
    R_g3 = R(g3bc[:, :], *[("xm", 1, c) for c in range(8)])
    wl = a.t("wl", [64, 256], BF16)
    alr = a.t("alr", [64, 256], BF16)
    gls = a.t("gls", [128, 256], BF16)
    v_tm = a.t("v_tm", [128, 2, D], BF16)
    g_tm = a.t("g_tm", [128, 2, D], BF16)
    y_tm = a.t("y_tm", [128, 2, D], BF16)
    TN = ("r_f", "k_f", "sg", "a_f", "kk", "sq", "t1", "kmod", "b_f", "cs", "Enc")
    _m5 = a.mark()
    tf = {n: a.t(n, [128, 256], F32) for n in TN}
    _m6 = a.mark()
    a.reset(_m5)
    us2 = a.t("us2", [128, D], F32)
    a.reset(_m6)
    ALIAS = {"rn": "sq", "kkn": "kk", "Ec": "sq", "btf": "b_f", "ktf": "t1", "bhf": "kk", "khf": "k_f", "Enp": "sg"}
    T = lambda n: R(tf[ALIAS.get(n, n)][:, :], ALIAS.get(n, n))
    sqb = a.t("sqb", [128, 256], BF16)
    ARs = [a.t("AR", [128, 2, 256], BF16) for _ in range(2)]
    bts = [a.t("bt", [128, 256], BF16) for _ in range(2)]
    prb = sqb
    bkhq = [[a.t("bkhq", [128, 2, 2, 128], BF16) for _ in range(2)] for _ in range(3)]
    ARm = [a.t("ARm", [128, 2, 2, 256], BF16) for _ in range(3)]
    encE = [a.t("encE", [128, 4], F32) for _ in range(3)]
    btms = [a.t("btm", [128, 2, 256], BF16) for _ in range(2)]
    ktms = [a.t("ktm", [128, 2, 256], BF16) for _ in range(2)]
    G1e = [[a.t("G1e", [128, 2, 256], BF16) for _ in range(2)] for _ in range(2)]
    G2e = [[a.t("G2e", [128, 2, 256], BF16) for _ in range(2)] for _ in range(2)]
    G3e = [a.t("G3e", [128, 2, 128], BF16) for _ in range(2)]
    _m3 = a.mark()
    PQ = [[a.t("PQ", [128, 4, 128], BF16) for _ in range(2)] for _ in range(2)]
    _m4 = a.mark()
    a.reset(_m3)
    ygT = a.t("ygT", [128, 8, 128], BF16)
    ygT2 = a.t("ygT2", [128, 8, 128], BF16)
    assert a.mark() == _m4
    a.reset(_m4)
    _tm0 = [a.t("Tm0", [128, 2, 128], BF16) for _ in range(2)]
    Tm = [[[_tm0[s_], a.t("Tm1", [128, 2, 128], BF16)] for s_ in range(2)] for _ in range(2)]
    X1s = a.t("X1s", [128, 2, 64], BF16)
    Us = a.t("Us", [128, 2, 64], BF16)
    rks = a.t("rks", [128, 2, 16], F32)
    gst = {n: a.t(n, [128, 16], F32) for n in ("gs1", "gs2", "gmean", "gvar")}
    a.reset(m0)
    for ver in range(3):
        for q in range(2):
            p.memset("dve", R(bkhq[ver][q][:, :, :, :], ("bkhq", ver, q)), 0.0)
        p.memset("pool", R(ARm[ver][:, :, :, :], ("ARma", ver), ("ARmr", ver)), 0.0)
    for par in range(2):
        p.memset("pool", R(btms[par][:, :, :], ("btm", par)), 0.0)
        p.memset("pool", R(ktms[par][:, :, :], ("ktm", par)), 0.0)
    identF = R(k.identF[:, :], "const")
    ones64 = R(cst[:, 4, 0:64], "const")
    MS2 = R(cst[:, 0, :].rearrange("p (h n) -> p h n", h=2), "const")
    MST2 = R(cst[:, 1, 0:256].rearrange("p (h n) -> p h n", h=2), "const")
    blk1 = R(cst[:, 2, 0:128], "const")
    ind2 = R(cst[:, 2, 128:130], "const")
    ST = k.ST
    STb = k.STb
    nun = ns // 2
    for un in range(nun):
        need_out = un in out_units
        if first_unit_of_core and un == 0:
            p.memset("dve", R(k.ulast[:, :, :], "ulast"), 0.0)
        p.copy("dve", R(uT[:, :, 0:1], "uT0"), R(k.ulast[:, :, :], "ulast"))
        for s in range(2):
            norm_transpose(k, p, hs(un * 2 + s), None,
                           R(uT[:, :, 1 + s * 128:1 + (s + 1) * 128], ("uT", s)), s, R(k.junk[0][:, :], ("junk", 0)), 2 + s,
                           gcm=11 + 2)
        p.copy("dve", R(k.ulast[:, :, :], "ulast"), R(uT[:, :, 256:257], ("uT", 1)))
        R_uTc = lambda c: R(uT[:, c, 1:257], ("uT", 0), ("uT", 1))
        p.tt("dve", R(xx[:, :, :], "xx"), R(uT[:, :, 0:256], "uT0", ("uT", 0), ("uT", 1)),
             R(uT[:, :, 1:257], ("uT", 0), ("uT", 1)), ALU.subtract)

        def mix(m, buf):
            for c in range(8):
                p.stt("dve" if c % 2 == 0 else "pool", R(xm[buf][:, c, :], ("xm", buf, c)), R(xx[:, c, :], "xx"),
                      CK(m, c), R_uTc(c), ALU.mult, ALU.add)
            return lambda c, sl=slice(0, 256): R(xm[buf][:, c, sl], ("xm", buf, c))

        xw = mix(1, 0)
        pw = ps_bank(k, 0, 256)
        for c in range(8):
            p.mm(pw[0:64, :], R(w1s[:, c, :], "w1s"), xw(c), start=(c == 0), stop=(c == 7))
        wlf = R(tf["r_f"][0:64, :], "r_f")
        sigmoid_act(p, wlf, pw[0:64, :], R(k.zero1[0:64, :], "const"), R(k.one1[0:64, :], "const"), scale=2.0)
        p.act(R(wl[:, :], "wl"), wlf, AF.Identity, bias=R(k.mone1[0:64, :], "const"), scale=2.0)
        xa = mix(4, 1)
        pa = ps_bank(k, 1, 256)
        for c in range(8):
            p.mm(pa[0:64, :], R(a1s[:, c, :], "a1s"), xa(c), start=(c == 0), stop=(c == 7))
        p.copy("act", R(alr[:, :], "alr"), pa[0:64, :])
        if need_out:
            xg = mix(5, 0)
            pg = ps_bank(k, 0, 256)
            for c in range(8):
                p.mm(pg, R(g1s[:, c, :], "g1s"), xg(c), start=(c == 0), stop=(c == 7))
            glf = R(tf["r_f"][:, :], "r_f")
            sigmoid_act(p, glf, pg, R(k.zero1[:, :], "const"), R(k.one1[:, :], "const"))
            p.copy("act", R(gls[:, :], "gls"), glf)
            for s in range(2):
                po = ps_pair(k, 1)
                for hf in range(2):
                    p.mm(po[:, hf * 512:(hf + 1) * 512], R(gls[:, s * 128:(s + 1) * 128], "gls"),
                         R(g2s[:, hf * 512:(hf + 1) * 512], "g2s"))
                p.copy("act", R(g_tm[:, s, :], ("g_tm", s)), po)
        xv = mix(3, 1)
        for s in range(2):
            po = ps_pair(k, 2 + s)
            for hf in range(2):
                for c in range(8):
                    p.mm(po[:, hf * 512:(hf + 1) * 512], xv(c, slice(s * 128, (s + 1) * 128)),
                         R(wts["wv"][:, c, hf * 512:(hf + 1) * 512], "wv"), start=(c == 0), stop=(c == 7))
            p.copy("act" if s == 0 else "dve", R(v_tm[:, s, :], ("v_tm", s)), po)
        xr = mix(0, 0) if need_out else None
        xk = mix(2, 1)
        def P(ct):
            par = ct % 3
            p2 = ct % 2
            AR, bt, btm, ktm = ARs[p2], bts[p2], btms[p2], ktms[p2]
            kARa, kARr, kbt, kbtm, kktm = ("ARa", p2), ("ARr", p2), ("bt", p2), ("btm", p2), ("ktm", p2)
            b0 = ps_bank(k, 0)
            b1 = ps_bank(k, 1)
            if need_out:
                for c in range(8):
                    p.mm(b0[:, 0:256], R(wts["wr"][:, c, ct * 128:(ct + 1) * 128], "wr"), xr(c), start=(c == 0), stop=(c == 7))
            for c in range(8):
                p.mm(b0[:, 256:512], R(wts["wk"][:, c, ct * 128:(ct + 1) * 128], "wk"), xk(c), start=(c == 0), stop=(c == 7))
            p.mm(b1[:, 0:256], R(w2s[:, ct * 128:(ct + 1) * 128], "w2s"), R(wl[:, :], "wl"))
            p.mm(b1[:, 256:512], R(a2s[:, ct * 128:(ct + 1) * 128], "a2s"), R(alr[:, :], "alr"))
            yield
            if need_out:
                p.copy("act", T("r_f"), b0[:, 0:256])
            p.copy("dve" if need_out else "act", T("k_f"), b0[:, 256:512])
            sigmoid_act(p, T("sg"), b1[:, 0:256], R(k.ncm[:, 0, ct:ct + 1], "const"), R(k.one1[:, :], "const"))
            yield
            sigmoid_act(p, T("a_f"), b1[:, 256:512], R(k.ncm[:, 1, ct:ct + 1], "const"), R(k.one1[:, :], "const"))
            yield
            p.ts("act_mul", T("kk"), T("k_f"), CK(8, ct), None, ALU.mult)
            p.tt("dve", R(sqb[:, :], "sqb"), T("kk"), T("kk"), ALU.mult)
            yield
            b7 = ps_bank(k, 0, 256)
            p.mm(b7, blk1, R(sqb[:, :], "sqb"))
            rsqrt_act(p, T("rn"), b7, R(k.eps24[:, :], "const"), 1.0)
            yield
            p.ts("pool", T("t1"), T("a_f"), -1.0, CK(9, ct), ALU.add, ALU.mult)
            yield
            p.tt("dve", T("kkn"), T("kk"), T("rn"), ALU.mult)
            p.stt("pool", T("kmod"), T("t1"), 1.0, T("k_f"), ALU.add, ALU.mult)
            yield
            p.tt("dve", T("b_f"), T("kkn"), T("a_f"), ALU.mult)
            for q in range(4):
                sl = slice(q * 64, (q + 1) * 64)
                p.scan(T("cs")[:, sl], ones64, T("sg")[:, sl], 0.0, ALU.mult, ALU.add)
                if q % 2:
                    yield
            p.act(T("Ec"), T("cs"), AF.Exp, scale=C_DEC)
            p.act(T("Enc"), T("cs"), AF.Exp, scale=-C_DEC)
            p.tt("dve", T("t1"), T("cs"), T("sg"), ALU.subtract)
            yield
            p.act(T("Enp"), T("t1"), AF.Exp, scale=-C_DEC)
            if need_out:
                p.tt("dve", R(AR[:, 1, :], kARr), T("r_f"), T("Enc"), ALU.mult)
            yield
            p.copy("act", R(encE[par][:, :], ("encE", par)), T("Enc")[:, 63:256:64])
            p.stt("dve", R(AR[:, 0, :], kARa), T("kkn"), -1.0, T("Enp"), ALU.mult, ALU.mult)
            yield
            p.tt("dve", T("btf"), T("b_f"), T("Ec"), ALU.mult)
            p.tt("dve", T("ktf"), T("kmod"), T("Ec"), ALU.mult)
            yield
            p.copy("act", R(bt[:, :], kbt), T("btf"))
            for hh in range(2):
                rows = slice(hh * 64, hh * 64 + 64)
                p.copy("act", R(btm[rows, hh, :], kbtm), T("btf")[rows, :])
                p.copy("act", R(ktm[rows, hh, :], kktm), T("ktf")[rows, :])
                yield
                p.copy("act", R(ARm[par][rows, hh, 0, :], ("ARma", par)), R(AR[rows, 0, :], kARa))
                if need_out:
                    p.copy("act", R(ARm[par][rows, hh, 1, :], ("ARmr", par)), R(AR[rows, 1, :], kARr))
                yield
            encb = R(encE[par][:, :].unsqueeze(2).broadcast_to([128, 4, 64]), ("encE", par))
            v4 = lambda r_: R(r_.ap.rearrange("p (q n) -> p q n", q=4), *r_.keys)
            p.tt("dve", v4(T("bhf")), v4(T("btf")), encb, ALU.mult)
            yield
            p.tt("pool", v4(T("khf")), v4(T("ktf")), encb, ALU.mult)
            yield
            b7f = ps_bank(k, 1)
            for ai, nm in enumerate(("bhf", "khf")):
                for s in range(2):
                    o = (ai * 2 + s) * 128
                    p.tr(b7f[:, o:o + 128], T(nm)[:, s * 128:(s + 1) * 128], identF)
            yield
            b7v = b7f.ap.rearrange("p (a s c) -> p a s c", a=2, s=2)
            p.copy("act", R(bkhq[par][0][0:64, :, :, :], ("bkhq", par, 0)), R(b7v[0:64], *b7f.keys))
            p.copy("act", R(bkhq[par][1][64:128, :, :, :], ("bkhq", par, 1)), R(b7v[64:128], *b7f.keys))
            yield
            if need_out:
                p.stt("dve", R(prb[:, :], "sqb"), T("r_f"), CK(10, ct), T("kmod"), ALU.mult, ALU.mult)
                for s in range(2):
                    p.mm(R(k.ps[3][:, 512 + s * 16 + 2 * ct: 512 + s * 16 + 2 * ct + 2], ("ps", 7)),
                         R(prb[:, s * 128:(s + 1) * 128], "sqb"), ind2)
            yield

        def I(ct, s):
            par = ct % 3
            p2 = ct % 2
            AR, bt, btm, ktm = ARs[p2], bts[p2], btms[p2], ktms[p2]
            kARa, kARr, kbt, kbtm, kktm = ("ARa", p2), ("ARr", p2), ("bt", p2), ("btm", p2), ("ktm", p2)
            tok = slice(s * 128, (s + 1) * 128)
            bankA = ps_bank(k, 2 + s)
            bankB = ps_bank(k, 4 + s)
            g1, g2, g3 = G1e[p2][s], G2e[p2][s], G3e[s]
            kG1, kG2, kG3 = ("G1e", p2, s), ("G2e", p2, s), ("G3e", s)
            nw = 256 if need_out else 128
            rhs = R(AR[:, :, tok], kARa, kARr) if need_out else R(AR[:, 0, tok], kARa)
            for hh in range(2):
                p.mm(bankA[:, hh * 256:hh * 256 + nw], R(btm[:, hh, tok], kbtm), rhs)
            yield
            p.tt("dve", R(g1[:, :, 0:nw], kG1), R(bankA.ap.rearrange("p (h n) -> p h n", h=2)[:, :, 0:nw], *bankA.keys), MS2[:, :, 0:nw], ALU.mult)
            for hh in range(2):
                p.mm(bankB[:, hh * 128:(hh + 1) * 128], R(ARm[par][:, hh, 0, tok], ("ARma", par)), R(bt[:, tok], kbt))
            yield
            p.tt("dve", R(g3[:, :, :], kG3), R(bankB.ap[:, 0:256].rearrange("p (h n) -> p h n", h=2), *bankB.keys), MST2, ALU.mult)
            for hh in range(2):
                p.mm(bankA[:, hh * 256:hh * 256 + nw], R(ktm[:, hh, tok], kktm), rhs)
            yield
            p.tt("dve", R(g2[:, :, 0:nw], kG2), R(bankA.ap.rearrange("p (h n) -> p h n", h=2)[:, :, 0:nw], *bankA.keys), MS2[:, :, 0:nw], ALU.mult)
            Aab = lambda hh: R(g1[:, hh, 0:128], kG1)
            tm = Tm[p2][s]
            tkey = lambda r_: ("Tm0", s) if r_ == 0 else ("Tm", p2, s, 1)
            for hh in range(2):
                p.tt("pool", R(tm[0][:, hh, :], ("Tm0", s)), Aab(hh), identF, ALU.add)
            yield
            Pp = [Aab(0), Aab(1)]
            Qp = [R(g3[:, 0, :], kG3), R(g3[:, 1, :], kG3)]
            tcur = 0
            for lvl in range(1, 6):
                pq = PQ[s][lvl % 2]
                kpq = ("PQ", s, lvl % 2)
                pT = bankA[:, 0:256]
                for hh in range(2):
                    if lvl <= 4:
                        p.mm(bankB[:, hh * 128:(hh + 1) * 128], Qp[hh], Pp[hh])
                    p.mm(bankB[:, (2 + hh) * 128:(3 + hh) * 128], Pp[hh], Qp[hh])
                yield
                lo = 0 if lvl <= 4 else 2
                p.copy("act", R(pq[:, lo:4, :], kpq),
                       R(bankB.ap.rearrange("p (a n) -> p a n", a=4)[:, lo:4, :], *bankB.keys))
                yield
                Pp = [R(pq[:, 0, :], kpq), R(pq[:, 1, :], kpq)]
                Qp = [R(pq[:, 2, :], kpq), R(pq[:, 3, :], kpq)]
                for hh in range(2):
                    p.mm(pT[:, hh * 128:(hh + 1) * 128], Qp[hh], R(tm[tcur][:, hh, :], tkey(tcur)))
                yield
                p.tt("dve", R(tm[1 - tcur][:, :, :], tkey(1 - tcur)),
                     R(pT.ap.rearrange("p (h n) -> p h n", h=2), *pT.keys), R(tm[tcur][:, :, :], tkey(tcur)), ALU.add)
                tcur = 1 - tcur
                yield
            assert tcur == 1

        def C(ct):
            par = ct % 3
            p2 = ct % 2
            for s in range(2):
                tok = slice(s * 128, (s + 1) * 128)
                g1, g2 = G1e[p2][s], G2e[p2][s]
                kG1, kG2 = ("G1e", p2, s), ("G2e", p2, s)
                Arb = lambda hh: R(g1[:, hh, 128:256], kG1)
                Aak = lambda hh: R(g2[:, hh, 0:128], kG2)
                Ark = lambda hh: R(g2[:, hh, 128:256], kG2)
                Tf = lambda hh: R(Tm[p2][s][1][:, hh, :], ("Tm", p2, s, 1))
                for q in range(2):
                    kb = slice(q * 64, q * 64 + 64)
                    pX = R(k.ps[3][:, 0:128].rearrange("p (h n) -> p h n", h=2), ("ps", 6))
                    pU = R(k.ps[3][:, 128:256].rearrange("p (h n) -> p h n", h=2), ("ps", 6))
                    pY = R(k.ps[3][:, 256:384].rearrange("p (h n) -> p h n", h=2), ("ps", 6))
                    pS = R(k.ps[3][:, 384:512].rearrange("p (h n) -> p h n", h=2), ("ps", 6))
                    for hh in range(2):
                        hc = slice((2 * ct + hh) * 64, (2 * ct + hh + 1) * 64)
                        p.mm(pX[:, hh, :], R(ARm[par][:, hh, 0, tok], ("ARma", par)), R(STb[:, ct, :], ("STb", ct)), start=True, stop=False)
                        p.mm(pX[:, hh, :], Aak(hh), R(v_tm[:, s, hc], ("v_tm", s)), start=False, stop=True)
                    yield
                    p.copy("act", R(X1s[:, :, :], "X1s"), pX)
                    yield
                    for hh in range(2):
                        p.mm(pU[:, hh, :], Tf(hh), R(X1s[:, hh, :], "X1s"))
                    yield
                    p.copy("act", R(Us[:, :, :], "Us"), pU)
                    yield
                    if need_out:
                        for hh in range(2):
                            hc = slice((2 * ct + hh) * 64, (2 * ct + hh + 1) * 64)
                            p.mm(pY[:, hh, :], R(ARm[par][:, hh, 1, tok], ("ARmr", par)), R(STb[:, ct, :], ("STb", ct)), start=True, stop=False)
                            p.mm(pY[:, hh, :], Arb(hh), R(Us[:, hh, :], "Us"), start=False, stop=False)
                            p.mm(pY[:, hh, :], Ark(hh), R(v_tm[:, s, hc], ("v_tm", s)), start=False, stop=True)
                    for hh in range(2):
                        hc = slice((2 * ct + hh) * 64, (2 * ct + hh + 1) * 64)
                        p.mm(pS[:, hh, :], R(bkhq[par][q][:, 0, s, :], ("bkhq", par, q)), R(Us[:, hh, :], "Us"), start=True, stop=False)
                        p.mm(pS[:, hh, :], R(bkhq[par][q][:, 1, s, :], ("bkhq", par, q)), R(v_tm[:, s, hc], ("v_tm", s)), start=False, stop=True)
                    yield
                    if need_out:
                        p.copy("act", R(y_tm[kb, s, 2 * ct * 64:(2 * ct + 2) * 64].rearrange("p (h n) -> p h n", h=2), ("y_tm", s)),
                               pY[kb, :, :])
                    ce = s * 2 + q
                    for hh in range(2):
                        pb = slice(hh * 64, hh * 64 + 64)
                        p.stt("dve", R(STb[pb, ct, :], ("STb", ct)), R(ST[pb, ct, :], ("ST", ct)),
                              R(encE[par][pb, ce:ce + 1], ("encE", par)), pS[pb, hh, :], ALU.mult, ALU.add)
                    yield
                    for hh in range(2):
                        pb = slice(hh * 64, hh * 64 + 64)
                        p.stt("dve", R(ST[pb, ct, :], ("ST", ct)), R(ST[pb, ct, :], ("ST", ct)),
                              R(encE[par][pb, ce:ce + 1], ("encE", par)), pS[pb, hh, :], ALU.mult, ALU.add)
                    yield

        def interleave(*gens):
            gens = [g for g in gens if g is not None]
            while gens:
                for g in list(gens):
                    try:
                        next(g)
                    except StopIteration:
                        gens.remove(g)

        interleave(P(0))
        for kk_ in range(9):
            interleave(I(kk_, 0) if kk_ < 8 else None, I(kk_, 1) if kk_ < 8 else None,
                       C(kk_ - 1) if kk_ >= 1 else None, P(kk_ + 1) if kk_ + 1 < 8 else None)
        if not need_out:
            continue
        p.copy("dve", R(rks[:, :, :], "rks"), R(k.ps[3][:, 512:544].rearrange("p (s h) -> p s h", s=2), ("ps", 7)))
        k.ngl = getattr(k, "ngl", 0) + 1
        p.dma("sp", R_gng, di["tmb"][0], f"gl{k.ngl % 2}a")
        p.dma("sp", R_gnb, di["tmb"][1], f"gl{k.ngl % 2}b")
        p.dma("sp", R_g3, di["gbc"][3], f"gl{k.ngl % 2}c")
        def OUT(s):
            ys = R(y_tm[:, s, :], ("y_tm", s))
            y3 = R(y_tm[:, s, :].rearrange("p (h n) -> p h n", h=16), ("y_tm", s))
            if s == 0:
                us_t, ukeys = k.junk[0], (("junk", 0),)
                yT, ykeys = ygT, (("PQ", 0, 0), ("PQ", 0, 1))
                gt = [gst[n][:, :] for n in ("gs1", "gs2", "gmean", "gvar")]
                gkeys = ["gs1", "gs2", "gmean", "gvar"]
            else:
                us_t, ukeys = us2, ("r_f", "k_f", "sg", "a_f")
                yT, ykeys = ygT2, (("PQ", 1, 0), ("PQ", 1, 1))
                gt = [tf["kk"][:, i * 16:(i + 1) * 16] for i in range(4)]
                gkeys = ["kk"] * 4
            us = R(us_t[:, :], *ukeys)
            us3 = R(us_t[:, :].rearrange("p (h n) -> p h n", h=16), *ukeys)
            gs1, gs2, gmean, gvar = [R(gt[i], gkeys[i]) for i in range(4)]
            bc = lambda r_: R(r_.ap.unsqueeze(2).broadcast_to([128, 16, 64]), *r_.keys)
            p.reduce("dve", gs1, y3, ALU.add)
            p.act(us, ys, AF.Square)
            yield
            p.reduce("dve", gs2, us3, ALU.add)
            p.ts("dve", gmean, gs1, 1.0 / 64, None, ALU.mult)
            yield
            p.tt("dve", gvar, gmean, gmean, ALU.mult)
            p.stt("dve", gvar, gs2, 1.0 / 64, gvar, ALU.mult, ALU.subtract)
            yield
            rsqrt_act(p, gvar, gvar, R(k.epsgn[:, :], "const"), 1.0)
            p.tt("dve", us3, y3, bc(gmean), ALU.subtract)
            yield
            p.tt("dve", us3, us3, bc(gvar), ALU.mult)
            v3 = R(v_tm[:, s, :].rearrange("p (h n) -> p h n", h=16), ("v_tm", s))
            p.tt("pool", y3, v3, bc(R(rks[:, s, :], "rks")), ALU.mult)
            yield
            p.tt("dve", us, us, R_gng, ALU.mult)
            yield
            p.tt("pool", us, us, R_gnb, ALU.add)
            yield
            p.tt("dve", us, us, ys, ALU.add)
            yield
            p.tt("dve", us, us, R(g_tm[:, s, :], ("g_tm", s)), ALU.mult)
            yield
            pt = ps_pair(k, 2 + s)
            for c in range(8):
                p.tr(pt[:, c * 128:(c + 1) * 128], us[:, c * 128:(c + 1) * 128], identF)
            yield
            p.copy("act", R(yT[:, :, :], *ykeys), R(pt.ap.rearrange("p (c t) -> p c t", c=8), *pt.keys))
            yield
            po = ps_pair(k, s)
            for hf in range(2):
                for c in range(8):
                    p.mm(po[:, hf * 512:(hf + 1) * 512], R(yT[:, c, :], *ykeys),
                         R(wts["wo"][:, c, hf * 512:(hf + 1) * 512], "wo"), start=(c == 0), stop=(c == 7))
            yield
            post_norm_residual(k, p, po, hs(un * 2 + s), R_g3, s, False, junk=us)
            yield

        interleave(OUT(0), OUT(1))


def kv_proj(k, p, slots, hs, shift):
    di = k.di
    a = k.al
    m0 = a.mark()
    wkv = a.t("wkv", [128, 8, 512], BF16)
    hnT = a.t("hnT", [128, 8, 128], BF16)
    usc = a.t("usc2", [128, D], F32)
    bvb = a.t("bvb", [128, 256], F32)
    a.reset(m0)
    p.dma("pool", R(wkv[:, :, :], "wkv"), di["w_kv"].rearrange("(dc p) f -> p dc f", p=128), "wkv")
    p.dma("sp", R(bvb[:, :], "bvb"), di["tmb"][3][:, 0:256], "par")
    if shift:
        n8 = k.ns_kv
        p.copy("dve", R(k.KTm[:, :, 0:128], ("kt", 0)), R(k.KTm[:, :, n8 * 128:(n8 + 1) * 128], ("kt", n8)))
        p.copy("pool", R(k.VA[:, 0, :, :], ("va", 0)), R(k.VA[:, n8, :, :], ("va", n8)))
    for j in slots:
        norm_transpose(k, p, hs(j), None, R(hnT[:, :, :], "hnT"), j % 2, R(usc[:, :], "usc2"), 2 + (j % 2), gcm=14)
        pk = ps_bank(k, 0, 256)
        for kt in range(2):
            for c in range(8):
                p.mm(pk[:, kt * 128:(kt + 1) * 128], R(wkv[:, c, kt * 128:(kt + 1) * 128], "wkv"),
                     R(hnT[:, c, :], "hnT"), start=(c == 0), stop=(c == 7))
        for kt in range(2):
            p.act(R(k.KTm[:, kt, (j + 1) * 128:(j + 2) * 128], ("kt", j + 1)),
                  pk[:, kt * 128:(kt + 1) * 128], AF.Identity, bias=R(k.cm[:, 12, kt:kt + 1], "const"))
        pv = ps_bank(k, 1, 256)
        for c in range(8):
            p.mm(pv, R(hnT[:, c, :], "hnT"), R(wkv[:, c, 256:512], "wkv"), start=(c == 0), stop=(c == 7))
        p.tt("dve", R(k.VA[:, j + 1, :, 0:64], ("va", j + 1)),
             R(pv.ap.rearrange("p (g n) -> p g n", g=4), *pv.keys),
             R(bvb[:, :].rearrange("p (g n) -> p g n", g=4), "bvb"), ALU.add)


def attention(k, p, nblk, hs, first_chunk_of_own):
    di = k.di
    a = k.al
    m0 = a.mark()
    wq = a.t("wq", [128, 8, D], BF16)
    wao = a.t("wao", [128, 8, D], BF16)
    uT1 = [a.t("uT1", [128, 8, 128], BF16) for _ in range(2)]
    qT = [a.t("qT", [128, 2, 8, 128], BF16) for _ in range(2)]
    usc = [a.t("usc3", [128, D], F32) for _ in range(2)]
    pex = [[a.t("pex", [128, 2, 512], BF16) for _ in range(2)] for _ in range(2)]
    o_tm = [a.t("o_tm", [128, D], F32) for _ in range(2)]
    oT = [a.t("oT", [128, 8, 128], BF16) for _ in range(2)]
    msb = [a.t("msb", [128, D], F32) for _ in range(2)]
    den = [a.t("den", [128, 4], F32) for _ in range(2)]
    bob = a.t("bob", [128, D], F32)
    g3b = a.t("g3b", [128, D], F32)
    esk = a.t("esk", [128, 16], F32)
    mfs = a.t("mfs", [128, 128], BF16)
    a.reset(m0)
    p.dma("pool", R(wq[:, :, :], "wq"), di["w_q"].rearrange("(dc p) f -> p dc f", p=128), "wq")
    p.dma("pool", R(wao[:, :, :], "wao"), di["w_ao"].rearrange("(dc p) f -> p dc f", p=128), "wao")
    p.dma("pool", R(mfs[:, :], "mfs"), di["mfirst"], "mfs")
    p.dma("sp", R(bob[:, :], "bob"), di["tmb"][2], "par")
    p.dma("sp", R(g3b[:, :], "g3b"), di["gbc"][9], "par")
    p.dma("sp", R(esk[:, :], "esk"), di["tmb"][3][:, 256:272], "par")
    p.act(R(esk[:, :], "esk"), R(esk[:, :], "esk"), AF.Exp)
    for par in range(2):
        p.memset("dve", R(qT[par][:, :, :, :], ("qT", par)), 0.0)
    identF = R(k.identF[:, :], "const")
    m_cur = R(k.cst[:, 3, 0:128].unsqueeze(1).broadcast_to([128, 4, 128]), "const")
    m_prev = R(k.cst[:, 3, 128:256].unsqueeze(1).broadcast_to([128, 4, 128]), "const")
    m_first = R(mfs[:, :].unsqueeze(1).broadcast_to([128, 4, 128]), "mfs")

    def block(b):
        par = b % 2
        kq, ku, ko, kot, kms, kdn = ("qT", par), ("uT1", par), ("o_tm", par), ("oT", par), ("msb", par), ("den", par)
        norm_transpose(k, p, hs(b), None, R(uT1[par][:, :, :], ku), b % 2, R(usc[par][:, :], ("usc3", par)), 2 + (b % 2), gcm=15)
        yield
        for t in range(8):
            pq = ps_bank(k, par, 128)
            for c in range(8):
                p.mm(pq, R(wq[:, c, t * 128:(t + 1) * 128], "wq"), R(uT1[par][:, c, :], ku), start=(c == 0), stop=(c == 7))
            for half in range(2):
                rows = slice(half * 64, half * 64 + 64)
                p.act(R(qT[par][rows, half, t, :], kq), pq[rows, :], AF.Identity, bias=R(k.cm[rows, 11, t:t + 1], "const"))
            yield
        for g in range(4):
            pp, half = g // 2, g % 2
            px = pex[par][g % 2]
            for kb, slot in ((0, b), (1, b + 1)):
                ps = ps_bank(k, 2 + par)
                p.mm(ps, R(k.KTm[:, pp, slot * 128:(slot + 1) * 128], ("kt", slot)),
                     R(qT[par][:, half, pp * 4:pp * 4 + 4, :], kq))
                pxr = R(px[:, kb, :], ("pex", par, g % 2, kb))
                p.act(pxr, ps, AF.Exp, scale=0.125)
                if kb == 1:
                    mk = m_cur
                elif b == 0 and first_chunk_of_own:
                    mk = m_first
                else:
                    mk = m_prev
                pxv = R(px[:, kb, :].rearrange("p (i q) -> p i q", i=4), ("pex", par, g % 2, kb))
                p.tt("dve" if kb == 0 else "pool", pxv, pxv, mk, ALU.mult)
                yield
            po = ps_bank(k, 4 + 2 * par + (g % 2), 260)
            for i in range(4):
                for kb, slot in ((0, b), (1, b + 1)):
                    p.mm(po[:, i * 65:(i + 1) * 65], R(px[:, kb, i * 128:(i + 1) * 128], ("pex", par, g % 2, kb)),
                         R(k.VA[:, slot, g, :], ("va", slot)), start=(kb == 0), stop=(kb == 1))
            yield
            po3 = R(po.ap.rearrange("p (i n) -> p i n", i=4), *po.keys)
            dn = R(den[par][:, :], kdn)
            p.tt("dve", dn, po3[:, :, 64], R(esk[:, 4 * g:4 * g + 4], "esk"), ALU.add)
            p.recip(dn, dn)
            p.tt("dve", R(o_tm[par][:, g * 256:(g + 1) * 256].rearrange("p (i n) -> p i n", i=4), ko),
                 po3[:, :, 0:64], R(den[par][:, :].unsqueeze(2).broadcast_to([128, 4, 64]), kdn), ALU.mult)
            yield
        pt = ps_pair(k, 2 + par)
        for c in range(8):
            p.tr(pt[:, c * 128:(c + 1) * 128], R(o_tm[par][:, c * 128:(c + 1) * 128], ko), identF)
        yield
        p.copy("act", R(oT[par][:, :, :], kot), R(pt.ap.rearrange("p (c t) -> p c t", c=8), *pt.keys))
        yield
        pm = ps_pair(k, 2 + par)
        for hf in range(2):
            for c in range(8):
                p.mm(pm[:, hf * 512:(hf + 1) * 512], R(oT[par][:, c, :], kot),
                     R(wao[:, c, hf * 512:(hf + 1) * 512], "wao"), start=(c == 0), stop=(c == 7))
        yield
        ms = R(msb[par][:, :], kms)
        p.tt("dve", ms, pm, R(bob[:, :], "bob"), ALU.add)
        yield
        post_norm_residual(k, p, ms, hs(b), R(g3b[:, :], "g3b"), b, False)
        yield

    def interleave(*gens):
        gens = list(gens)
        while gens:
            for g_ in list(gens):
                try:
                    next(g_)
                except StopIteration:
                    gens.remove(g_)

    for b in range(0, nblk, 2):
        if b + 1 < nblk:
            interleave(block(b), block(b + 1))
        else:
            interleave(block(b))


def full_cfg():
    chunks = [
        dict(sub0=0, nsub=8, out0=0, out_units=[], kv_slots=[], kv_shift=False, own=False, first_own=False),
        dict(sub0=8, nsub=8, out0=0, out_units=[3], kv_slots=[7], kv_shift=False, own=False, first_own=False),
        dict(sub0=16, nsub=8, out0=0, out_units=[0, 1, 2, 3], kv_slots=list(range(8)), kv_shift=True, own=True, first_own=True),
        dict(sub0=24, nsub=8, out0=8, out_units=[0, 1, 2, 3], kv_slots=list(range(8)), kv_shift=True, own=True, first_own=False),
    ]
    return dict(chunks=chunks, ntw=4096, nout=2048, stages="full")


def kernel(**inputs):
    g = host_prep(inputs)
    x = np.asarray(inputs["x"], np.float32)
    B, T, _ = x.shape
    ii = np.arange(128)
    mprev = (ii[:, None] > ii[None, :]).astype(np.float32)
    in_maps = []
    for core in range(8):
        b, half = core // 2, core % 2
        m = dict(g)
        if half == 0:
            xw = np.concatenate([np.zeros((2048, D), np.float32), x[b, 0:2048]], 0)
            m["mfirst"] = np.zeros((128, 128), np.float32)
        else:
            xw = x[b]
            m["mfirst"] = mprev
        m["xw"] = np.ascontiguousarray(xw)
        in_maps.append(m)
    nc, p = build(full_cfg())
    res = run_bass_kernel_spmd(nc, in_maps, core_ids=list(range(8)))
    out = np.zeros((B, T, D), np.float32)
    for core in range(8):
        b, half = core // 2, core % 2
        out[b, half * 2048:(half + 1) * 2048] = res.results[core]["out"]
    return out
```

```python
import numpy as np
from contextlib import ExitStack
import concourse.bass as bass
import concourse.mybir as mybir
from concourse.bass_utils import run_bass_kernel_spmd

F32 = mybir.dt.float32
BF16 = mybir.dt.bfloat16
AF = mybir.ActivationFunctionType
ALU = mybir.AluOpType
AX = mybir.AxisListType

D = 1024
DFF = 2816
NFC = 22
RMS_EPS = 1e-6
GN_EPS = 64e-5
SAME_ENGINE_SYNC = True


class R:
    __slots__ = ("ap", "keys")

    def __init__(self, ap, *keys):
        self.ap = ap
        self.keys = tuple(keys)

    def __getitem__(self, sl):
        return R(self.ap[sl], *self.keys)


class Op:
    __slots__ = ("eng", "fn", "deps", "signal", "semval", "dma_sem", "idx", "is_dma")

    def __init__(self, eng, fn, is_dma, dma_sem):
        self.eng = eng
        self.fn = fn
        self.deps = {}
        self.signal = False
        self.semval = 0
        self.dma_sem = dma_sem
        self.is_dma = is_dma


class Prog:
    COMPUTE = ("pe", "act", "dve", "pool")

    def __init__(self, nc):
        self.nc = nc
        self.ops = []
        self.last_w = {}
        self.readers = {}

    def add(self, eng, fn, reads=(), writes=(), dma_sem=None):
        op = Op(eng, fn, dma_sem is not None, dma_sem)
        op.idx = len(self.ops)
        deps = {}
        for k in reads:
            w = self.last_w.get(k)
            if w is not None:
                deps[w] = "w"
        for k in writes:
            w = self.last_w.get(k)
            if w is not None:
                deps[w] = "w"
            for r in self.readers.get(k, ()):
                if r not in deps:
                    deps[r] = "r"
        for k in reads:
            if isinstance(k, tuple) and k and k[0] == "ps":
                for r in self.readers.get(k, ()):
                    if r not in deps:
                        deps[r] = "x"
        deps.pop(op, None)
        for k in reads:
            self.readers.setdefault(k, []).append(op)
        for k in writes:
            self.last_w[k] = op
            self.readers[k] = []
        for d, kind in deps.items():
            if d.is_dma or op.is_dma:
                need = True
                if d.is_dma and op.is_dma and False:
                    need = True
            elif d.eng == op.eng:
                if op.eng == "pe":
                    need = False
                else:
                    need = SAME_ENGINE_SYNC and kind == "w"
            else:
                need = True
            if need:
                op.deps[d] = kind
                d.signal = True
        self.ops.append(op)
        return op

    def emit(self, es):
        nc = self.nc
        sems = {}
        LIM = 16000
        counts = {e: 0 for e in self.COMPUTE}
        dcounts = {}
        for op in self.ops:
            if op.is_dma:
                if op.dma_sem not in sems:
                    sems[op.dma_sem] = es.enter_context(nc.semaphore("dsem_" + op.dma_sem))
                    dcounts[op.dma_sem] = 0
                dcounts[op.dma_sem] += 16
                op.semval = dcounts[op.dma_sem]
            elif op.signal:
                counts[op.eng] += 1
                ep = (counts[op.eng] - 1) // LIM
                op.dma_sem = f"{op.eng}{ep}"
                if op.dma_sem not in sems:
                    sems[op.dma_sem] = es.enter_context(nc.semaphore("sem_" + op.dma_sem))
                op.semval = counts[op.eng] - ep * LIM
        self.stats = dict(counts=counts, dcounts=dcounts, nops=len(self.ops))
        by_eng = {}
        for op in self.ops:
            by_eng.setdefault(op.eng, []).append(op)
        block = es.enter_context(nc.Block())
        final_waits = [(sems[s], v) for s, v in dcounts.items() if s.startswith("out")]

        def run(eng_name, e):
            waited = {}
            for op in by_eng.get(eng_name, []):
                need = {}
                for d in op.deps:
                    s = d.dma_sem
                    if d.semval > need.get(s, 0):
                        need[s] = d.semval
                for s, v in need.items():
                    if waited.get(s, 0) >= v:
                        continue
                    e.wait_ge(sems[s], v)
                    waited[s] = v
                inst = op.fn(e)
                if op.is_dma:
                    inst.then_inc(sems[op.dma_sem], 16)
                elif op.signal:
                    inst.then_inc(sems[op.dma_sem], 1)
            if eng_name == "sp":
                for s, v in final_waits:
                    e.wait_ge(s, v)

        @block.tensor
        def _(e):
            run("pe", e)

        @block.scalar
        def _(e):
            run("act", e)

        @block.vector
        def _(e):
            run("dve", e)

        @block.gpsimd
        def _(e):
            run("pool", e)

        @block.sync
        def _(e):
            run("sp", e)

    @staticmethod
    def _keys(*rs):
        ks = []
        for r in rs:
            if isinstance(r, R):
                ks.extend(r.keys)
        return ks

    @staticmethod
    def _ap(x):
        return x.ap if isinstance(x, R) else x

    def mm(self, out, lhsT, rhs, start=True, stop=True):
        o, l, r = out.ap, lhsT.ap, rhs.ap
        return self.add("pe", lambda e: e.matmul(o, l, r, start=start, stop=stop),
                        reads=self._keys(lhsT, rhs), writes=out.keys)

    def tr(self, out, in_, ident):
        o, i, d = out.ap, in_.ap, ident.ap
        return self.add("pe", lambda e: e.transpose(o, i, d),
                        reads=self._keys(in_, ident), writes=out.keys)

    def act(self, out, in_, func, bias=None, scale=1.0, accum=None, eng="act"):
        o, i = out.ap, in_.ap
        b = self._ap(bias) if bias is not None else None
        sc = self._ap(scale)
        ac = accum.ap if accum is not None else None
        kw = {}
        if b is not None:
            kw["bias"] = b
        if ac is not None:
            kw["accum_out"] = ac
        return self.add(eng, lambda e: e.activation(out=o, in_=i, func=func, scale=sc, **kw),
                        reads=self._keys(in_, bias, scale),
                        writes=list(out.keys) + (list(accum.keys) if accum is not None else []))

    def tt(self, eng, out, in0, in1, op):
        o, a, b = out.ap, in0.ap, in1.ap
        return self.add(eng, lambda e: e.tensor_tensor(o, a, b, op),
                        reads=self._keys(in0, in1), writes=out.keys)

    def ts(self, eng, out, in0, s1, s2, op0, op1=None):
        o, a = out.ap, in0.ap
        x1, x2 = self._ap(s1), self._ap(s2)
        if eng == "act_mul":
            return self.add("act", lambda e: e.mul(o, a, x1), reads=self._keys(in0, s1), writes=out.keys)
        eng = "dve"
        if op1 is None:
            fn = lambda e: e.tensor_scalar(o, a, x1, None, op0)
        else:
            fn = lambda e: e.tensor_scalar(o, a, x1, x2, op0, op1)
        return self.add(eng, fn, reads=self._keys(in0, s1, s2), writes=out.keys)

    def stt(self, eng, out, in0, scalar, in1, op0, op1):
        o, a, b = out.ap, in0.ap, in1.ap
        s = self._ap(scalar)
        eng = "dve"
        return self.add(eng, lambda e: e.scalar_tensor_tensor(o, a, s, b, op0, op1),
                        reads=self._keys(in0, scalar, in1), writes=out.keys)

    def copy(self, eng, out, in_):
        o, i = out.ap, in_.ap
        if eng == "act":
            return self.add(eng, lambda e: e.copy(o, i), reads=in_.keys, writes=out.keys)
        return self.add(eng, lambda e: e.tensor_copy(o, i), reads=in_.keys, writes=out.keys)

    def memset(self, eng, out, val):
        o = out.ap
        return self.add(eng, lambda e: e.memset(o, val), reads=(), writes=out.keys)

    def recip(self, out, in_):
        o, i = out.ap, in_.ap
        return self.add("dve", lambda e: e.reciprocal(o, i), reads=in_.keys, writes=out.keys)

    def reduce(self, eng, out, in_, op, axis=AX.X):
        o, i = out.ap, in_.ap
        return self.add(eng, lambda e: e.tensor_reduce(o, i, axis, op), reads=in_.keys, writes=out.keys)

    def scan(self, out, d0, d1, init, op0, op1):
        o, a, b = out.ap, d0.ap, d1.ap
        return self.add("dve", lambda e: e.tensor_tensor_scan(o, a, b, init, op0, op1),
                        reads=self._keys(d0, d1), writes=out.keys)

    def dma(self, eng, out, in_, sem):
        o, i = self._ap(out), self._ap(in_)
        if sem == "par":
            self._npar = getattr(self, "_npar", 0) + 1
            sem = f"par{self._npar}"
        return self.add(eng, lambda e: e.dma_start(out=o, in_=i),
                        reads=self._keys(in_), writes=self._keys(out), dma_sem=sem)

    def barrier(self):
        last = {}
        dmas = []
        for op in self.ops[self._bar_from:]:
            if op.is_dma:
                dmas.append(op)
            else:
                last[op.eng] = op
        self._bar_from = len(self.ops)
        deps = list(last.values()) + dmas
        prev = getattr(self, "_pending", {})
        self._pending = {e: list(deps) + prev.get(e, []) for e in ("pe", "act", "dve", "pool", "sp")}


_orig_add = Prog.add


def _add_with_barrier(self, eng, fn, reads=(), writes=(), dma_sem=None):
    op = _orig_add(self, eng, fn, reads, writes, dma_sem)
    pend = getattr(self, "_pending", None)
    if pend and eng in pend:
        for d in pend.pop(eng):
            if d is op:
                continue
            if (not d.is_dma) and d.eng == eng and eng == "pe":
                continue
            op.deps[d] = "w"
            d.signal = True
    return op


Prog.add = _add_with_barrier
Prog._bar_from = 0


class Alloc:
    def __init__(self, nc, base=16512, limit=229344):
        self.nc = nc
        self.off = base
        self.limit = limit
        self.n = 0

    def mark(self):
        return self.off

    def reset(self, off):
        self.off = off

    def t(self, name, shape, dtype):
        size = int(np.prod(shape[1:])) * (4 if dtype == F32 else 2)
        self.off = (self.off + 63) // 64 * 64
        assert self.off + size <= self.limit, (name, self.off, size)
        self.n += 1
        h = self.nc.alloc_sbuf_tensor_at(f"{name}_{self.n}", list(shape), dtype, offset=self.off)
        self.off += size
        return h


class K:
    pass


def ps_bank(k, b, n=512):
    t = k.ps[b // 2]
    off = (b % 2) * 512
    return R(t[:, off:off + n], ("ps", b))


def ps_pair(k, i):
    return R(k.ps[i][:, :], ("ps", 2 * i), ("ps", 2 * i + 1))


def rstd_from_ss(k, p, ss, rstd, n, scale, bias_tile):
    p.act(rstd[:, 0:n], ss[:, 0:n], AF.Sqrt, bias=bias_tile, scale=scale)
    p.recip(rstd[:, 0:n], rstd[:, 0:n])


def ffn(k, p, srcs, dsts, layer, which, wsem_base):
    G = len(srcs)
    NT = G * 128
    a = k.al
    m0 = a.mark()
    xnT = a.t("xnT", [128, 8, NT], BF16)
    hidT = a.t("hidT", [128, NFC, NT], BF16)
    win = [a.t("win", [128, 8, 2, 256], BF16) for _ in range(2)]
    wout = a.t("wout", [128, NFC, 1024], BF16)
    xn = [a.t("xn", [128, 1024], F32) for _ in range(2)]
    sg = [a.t("sg", [128, 512], F32) for _ in range(2)]
    tmp = [a.t("tmp", [128, 1024], F32) for _ in range(2)]
    gpre = a.t("gpre", [128, 1024], F32)
    gpost = a.t("gpost", [128, 1024], F32)
    a.reset(m0)
    tag = f"f{layer}{which}"
    R_xnT = lambda s: R(xnT[:, :, s * 128:(s + 1) * 128], ("xnT", s))
    gi = 0 if which == 0 else 4
    R_gpre = R(gpre[:, :], "gpre")
    R_gpost = R(gpost[:, :], "gpost")
    p.dma("sp", R_gpre, k.gbc[layer * 6 + gi], "par")
    p.dma("sp", R_gpost, k.gbc[layer * 6 + gi + 1], "par")
    ss = R(k.ss[:, :], "ss")
    rstd = R(k.rstd[:, :], "rstd")
    w_in_v = k.ffn_w_in[layer, which].rearrange("(dc p) (two f) -> p dc two f", p=128, two=2)
    w_out_v = k.ffn_w_out[layer, which].rearrange("(fc p) d -> p fc d", p=128)
    NFG = 11
    R_win = lambda s: R(win[s][:, :, :, :], ("win", s))
    win_issued = [0]

    def issue_win(fg):
        s = fg % 2
        for two in range(2):
            p.dma("pool", R(win[s][:, :, two, :], ("win", s, two)),
                  w_in_v[:, :, two, fg * 256:(fg + 1) * 256], f"win{s}{two}")

    issue_win(0)
    issue_win(1)
    for s in range(G):
        junk = R(tmp[s % 2][:, :], ("tmp", s % 2))
        p.act(junk, srcs[s], AF.Square, accum=ss[:, s:s + 1])
    rstd_from_ss(k, p, ss, rstd, G, 1.0 / D, R(k.eps1[:, :], "const"))
    for s in range(G):
        xs = R(xn[s % 2][:, :], ("xn", s % 2))
        p.stt("dve", xs, srcs[s], rstd[:, s:s + 1], R_gpre, ALU.mult, ALU.mult)
        pt = ps_pair(k, 2 + (s % 2))
        for c in range(8):
            p.tr(pt[:, c * 128:(c + 1) * 128], xs[:, c * 128:(c + 1) * 128], R(k.identF[:, :], "const"))
        eng = "act" if s % 2 == 0 else "dve"
        p.copy(eng, R_xnT(s), R(pt.ap.rearrange("p (c t) -> p c t", c=8), *pt.keys))
    R_wout = lambda hlf: R(wout[:, hlf * 11:(hlf + 1) * 11, :], ("wout", hlf))
    for hlf in range(2):
        p.dma("pool", R_wout(hlf), w_out_v[:, hlf * 11:(hlf + 1) * 11, :], f"wout{hlf}")
    nblocks = [(i, min(i + 512, NT)) for i in range(0, NT, 512)]
    it = 0
    for fg in range(NFG):
        s_w = fg % 2
        for jj in range(2):
            j = fg * 2 + jj
            for (n0, n1) in nblocks:
                nn = n1 - n0
                set_ = it % 2
                it += 1
                pg = ps_bank(k, set_ * 2, nn)
                pu = ps_bank(k, set_ * 2 + 1, nn)
                rhs_keys = [("xnT", s) for s in range(n0 // 128, n1 // 128)]
                for two, pt in ((0, pg), (1, pu)):
                    for dc in range(8):
                        p.mm(pt, R(win[s_w][:, dc, two, jj * 128:(jj + 1) * 128], ("win", s_w, two)),
                             R(xnT[:, dc, n0:n1], *rhs_keys), start=(dc == 0), stop=(dc == 7))
                sgt = R(sg[set_][:, 0:nn], ("sg", set_))
                p.act(sgt, pg, AF.Silu)
                p.tt("dve", R(hidT[:, j, n0:n1], ("hid", j)), sgt, pu, ALU.mult)
        if fg + 2 < NFG:
            issue_win(fg + 2)
    for s in range(G):
        po = ps_pair(k, s % 2)
        for dh in range(2):
            for j in range(NFC):
                p.mm(po[:, dh * 512:(dh + 1) * 512], R(hidT[:, j, s * 128:(s + 1) * 128], ("hid", j)),
                     R(wout[:, j, dh * 512:(dh + 1) * 512], ("wout", j // 11)),
                     start=(j == 0), stop=(j == NFC - 1))
        ss2 = R(k.ss2[:, :], ("ss2", s % 2))
        rs2 = R(k.rstd2[:, :], ("rs2", s % 2))
        junk = R(tmp[s % 2][:, :], ("tmp", s % 2))
        p.act(junk, po, AF.Square, accum=ss2[:, s % 2:s % 2 + 1])
        p.act(rs2[:, s % 2:s % 2 + 1], ss2[:, s % 2:s % 2 + 1], AF.Sqrt, bias=R(k.eps4[:, :], "const"), scale=4.0 / D)
        p.recip(rs2[:, s % 2:s % 2 + 1], rs2[:, s % 2:s % 2 + 1])
        p.stt("dve", junk, po, rs2[:, s % 2:s % 2 + 1], R_gpost, ALU.mult, ALU.mult)
        p.tt("pool", dsts[s], srcs[s], junk, ALU.add)


def declare_inputs(nc, NTW, NOUT):
    di = {}

    def inp(name, shape):
        di[name] = nc.dram_tensor(name, list(shape), F32, kind="ExternalInput").ap()

    inp("xw", [NTW, D])
    inp("gbc", [13, 128, D])
    inp("ffn_w_in", [2, 2, D, 2 * DFF])
    inp("ffn_w_out", [2, 2, DFF, D])
    inp("w_rkv", [3, D, D])
    inp("w_ro", [D, D])
    inp("w1", [D, 64])
    inp("w2", [64, D])
    inp("a1", [D, 64])
    inp("a2", [64, D])
    inp("g1", [D, 128])
    inp("g2", [128, D])
    inp("cm", [128, 16, 8])
    inp("tmb", [4, 128, D])
    inp("w_kv", [D, 512])
    inp("w_q", [D, D])
    inp("w_ao", [D, D])
    inp("consts", [6, 128, 512])
    inp("mfirst", [128, 128])
    di["out"] = nc.dram_tensor("out", [NOUT, D], F32, kind="ExternalOutput").ap()
    return di


def build(cfg):
    nc = bass.Bass("TRN2", target_bir_lowering=False)
    chunks = cfg["chunks"]
    NTW = cfg["ntw"]
    NOUT = cfg["nout"]
    di = declare_inputs(nc, NTW, NOUT)
    k = K()
    k.nc = nc
    k.di = di
    k.gbc = [R(di["gbc"][i], ) for i in range(13)]
    k.ffn_w_in = di["ffn_w_in"]
    k.ffn_w_out = di["ffn_w_out"]
    es = ExitStack()
    k.ps = [es.enter_context(nc.psum_tensor(f"ps{i}", [128, 1024], F32)) for i in range(4)]
    al = Alloc(nc)
    k.al = al
    p = Prog(nc)
    k.p = p
    k.h = al.t("h", [128, 8, D], F32)
    k.identF = al.t("identF", [128, 128], F32)
    k.cst = al.t("cst", [128, 5, 512], BF16)
    k.eps1 = al.t("eps1", [128, 1], F32)
    k.eps4 = al.t("eps4", [128, 1], F32)
    k.ss = al.t("ss", [128, 16], F32)
    k.rstd = al.t("rstd", [128, 16], F32)
    k.ss2 = al.t("ss2", [128, 2], F32)
    k.rstd2 = al.t("rstd2", [128, 2], F32)
    _j = al.t("junk", [128, D], F32)
    k.junk = [_j, _j]
    k.ST = al.t("ST", [128, 8, 64], F32)
    k.STb = al.t("STb", [128, 8, 64], BF16)
    k.ulast = al.t("ulast", [128, 8, 1], F32)
    k.eps24 = al.t("eps24", [128, 1], F32)
    k.epsgn = al.t("epsgn", [128, 1], F32)
    k.cm = al.t("cm", [128, 16, 8], F32)
    p.dma("sp", R(k.identF[:, :], "const"), di["consts"][0][:, 0:128], "par")
    p.dma("pool", R(k.cst[:, :, :], "const"), di["consts"][1:6].rearrange("a p n -> p a n"), "parc")
    p.dma("sp", R(k.cm[:, :, :], "const"), di["cm"], "par")
    k.one1 = al.t("one1", [128, 1], F32)
    k.zero1 = al.t("zero1", [128, 1], F32)
    k.mone1 = al.t("mone1", [128, 1], F32)
    k.ncm = al.t("ncm", [128, 2, 8], F32)
    p.memset("dve", R(k.one1[:, :], "const"), 1.0)
    p.memset("dve", R(k.zero1[:, :], "const"), 0.0)
    p.memset("dve", R(k.mone1[:, :], "const"), -1.0)
    k.hm = al.t("hm", [128, 2], F32)
    p.copy("dve", R(k.hm[:, :], "const"), R(k.cst[:, 2, 128:130], "const"))
    k.KTm = al.t("KTm", [128, 2, 1152], BF16)
    k.VA = al.t("VA", [128, 9, 4, 65], BF16)
    p.memset("dve", R(k.KTm[:, :, :], *[("kt", j) for j in range(9)]), 0.0)
    p.memset("dve", R(k.VA[:, :, :, :], *[("va", j) for j in range(9)]), 1.0)
    p.ts("dve", R(k.ncm[:, :, :], "const"), R(k.cm[:, 6:8, :], "const"), -1.0, None, ALU.mult)
    p.memset("dve", R(k.eps24[:, :], "const"), 1e-24)
    p.memset("dve", R(k.epsgn[:, :], "const"), GN_EPS)
    p.memset("dve", R(k.ST[:, :, :], *[("ST", c) for c in range(8)]), 0.0)
    p.memset("dve", R(k.STb[:, :, :], *[("STb", c) for c in range(8)]), 0.0)
    p.memset("dve", R(k.eps1[:, :], "const"), RMS_EPS)
    p.memset("dve", R(k.eps4[:, :], "const"), 4.0 * RMS_EPS)
    hs = lambda j: R(k.h[:, j, :], ("h", j))
    xw = di["xw"].rearrange("(n p) d -> n p d", p=128)
    outv = di["out"].rearrange("(n p) d -> n p d", p=128)
    stages = cfg["stages"]
    for ci, ch in enumerate(chunks):
        ns = ch["nsub"]
        k.ns_kv = ns
        for j in range(ns):
            p.dma("sp", hs(j), xw[ch["sub0"] + j], f"x{j}")
        ffn(k, p, [hs(j) for j in range(ns)], [hs(j) for j in range(ns)], 0, 0, "a")
        p.barrier()
        if stages == "ffn1":
            for j in range(ns):
                p.dma("sp", outv[ch["out0"] + j], hs(j), "out")
            continue
        rwkv_chunk(k, p, ns, hs, set(ch["out_units"]), ci == 0)
        p.barrier()
        if stages == "rwkv":
            for j in range(ns):
                p.dma("sp", outv[ch["out0"] + j], hs(j), "out")
            continue
        kvs = ch["kv_slots"]
        if kvs:
            ffn(k, p, [hs(j) for j in kvs], [hs(j) for j in kvs], 0, 1, "b")
            p.barrier()
            kv_proj(k, p, kvs, hs, ch["kv_shift"])
            p.barrier()
        if not ch["own"]:
            continue
        if stages == "kv":
            for j in range(ns):
                p.dma("sp", outv[ch["out0"] + j], hs(j), "out")
            continue
        ffn(k, p, [hs(j) for j in range(ns)], [hs(j) for j in range(ns)], 1, 0, "c")
        p.barrier()
        attention(k, p, ns, hs, ch["first_own"])
        p.barrier()
        if stages == "attn":
            for j in range(ns):
                p.dma("sp", outv[ch["out0"] + j], hs(j), "out")
            continue
        ffn(k, p, [hs(j) for j in range(ns)], [hs(j) for j in range(ns)], 1, 1, "d")
        p.barrier()
        for j in range(ns):
            p.dma("sp", outv[ch["out0"] + j], hs(j), "out")
    p.emit(es)
    es.close()
    return nc, p


def host_prep(inputs):
    f = np.float32
    g = {}
    ng = np.asarray(inputs["norm_g"], f)
    gb = np.concatenate([ng.reshape(12, D), np.asarray(inputs["kv_norm_g"], f).reshape(1, D)], 0)
    g["gbc"] = np.ascontiguousarray(np.broadcast_to(gb[:, None, :], (13, 128, D)))
    g["ffn_w_in"] = np.asarray(inputs["ffn_w_in"], f)
    g["ffn_w_out"] = np.asarray(inputs["ffn_w_out"], f)
    g["w_rkv"] = np.asarray(inputs["rwkv_w_rkv"], f)[0]
    g["w_ro"] = np.asarray(inputs["rwkv_w_o"], f)[0]
    for nm in ("w1", "w2", "a1", "a2", "g1", "g2"):
        g[nm] = np.asarray(inputs["rwkv_" + nm], f)[0]
    cm = np.zeros((16, D), f)
    cm[0:6] = np.asarray(inputs["rwkv_mu"], f)[0]
    cm[6] = np.asarray(inputs["rwkv_w0"], f)[0]
    cm[7] = np.asarray(inputs["rwkv_a0"], f)[0]
    cm[8] = np.asarray(inputs["rwkv_k_k"], f)[0]
    cm[9] = np.asarray(inputs["rwkv_k_a"], f)[0]
    cm[10] = np.asarray(inputs["rwkv_r_k"], f)[0].reshape(D)
    bq = np.asarray(inputs["attn_b_q"], f)[0].reshape(16, 64)
    perm = []
    for t in range(8):
        pp, i = t // 4, t % 4
        perm += [8 * pp + i, 8 * pp + 4 + i]
    cm[11] = bq[perm].reshape(D)
    bkv = np.asarray(inputs["b_kv"], f)
    cm[12, 0:256] = bkv[0:256]
    cm[13] = ng[0, 2]
    cm[14] = np.asarray(inputs["kv_norm_g"], f)
    cm[15] = ng[1, 2]
    g["cm"] = np.ascontiguousarray(cm.reshape(16, 8, 128).transpose(2, 0, 1))
    tmb = np.zeros((4, D), f)
    tmb[0] = np.asarray(inputs["rwkv_gn_g"], f)[0]
    tmb[1] = np.asarray(inputs["rwkv_gn_b"], f)[0]
    tmb[2] = np.asarray(inputs["attn_b_o"], f)[0]
    tmb[3, 0:256] = bkv[256:512]
    tmb[3, 256:272] = np.asarray(inputs["attn_sinks"], f)[0]
    g["tmb"] = np.ascontiguousarray(np.broadcast_to(tmb[:, None, :], (4, 128, D)))
    g["w_kv"] = np.asarray(inputs["w_kv"], f)
    wqh = np.asarray(inputs["attn_w_q"], f)[0].reshape(D, 16, 64)
    g["w_q"] = np.ascontiguousarray(wqh[:, perm, :].reshape(D, D))
    g["w_ao"] = np.asarray(inputs["attn_w_o"], f)[0]
    cs = np.zeros((6, 128, 512), f)
    cs[0, :, 0:128] = np.eye(128, dtype=f)
    ii = np.arange(128)
    same = (ii[:, None] // 64) == (ii[None, :] // 64)
    MS = ((ii[:, None] < ii[None, :]) & same).astype(f)
    MI = ((ii[:, None] <= ii[None, :]) & same).astype(f)
    cs[1] = np.concatenate([MS, MI, MS, MI], 1)
    cs[2, :, 0:256] = np.concatenate([MS.T, MS.T], 1)
    cs[3, :, 0:128] = same.astype(f)
    cs[3, :, 128] = (ii < 64)
    cs[3, :, 129] = (ii >= 64)
    cs[4, :, 0:128] = (ii[:, None] <= ii[None, :])
    cs[4, :, 128:256] = (ii[:, None] > ii[None, :])
    cs[5, :, 0:64] = 1.0
    g["consts"] = cs
    return g


C_DEC = 0.6065306597126334


def rsqrt_act(p, out, in_, bias, scale):
    p.act(out, in_, AF.Ln, bias=bias, scale=scale)
    p.act(out, out, AF.Exp, scale=-0.5)


def sigmoid_act(p, out, in_, nbias, ones, scale=1.0):
    p.act(out, in_, AF.Exp, bias=nbias, scale=-scale)
    p.act(out, out, AF.Ln, bias=ones, scale=1.0)
    p.act(out, out, AF.Exp, scale=-1.0)


def post_norm_residual(k, p, po, hdst, gR, s, half, junk=None):
    ss2 = R(k.ss2[:, :], ("ss2", s % 2))
    rs2 = R(k.rstd2[:, :], ("rs2", s % 2))
    if junk is None:
        junk = R(k.junk[s % 2][:, :], ("junk", 0))
    c = s % 2
    p.act(junk, po, AF.Square, accum=ss2[:, c:c + 1])
    if half:
        rsqrt_act(p, rs2[:, c:c + 1], ss2[:, c:c + 1], R(k.eps4[:, :], "const"), 4.0 / D)
    else:
        rsqrt_act(p, rs2[:, c:c + 1], ss2[:, c:c + 1], R(k.eps1[:, :], "const"), 1.0 / D)
    p.stt("dve", junk, po, rs2[:, c:c + 1], gR, ALU.mult, ALU.mult)
    p.tt("pool", hdst, hdst, junk, ALU.add)


def norm_transpose(k, p, src, gR, dstT, s, scratch, pair, gcm=None):
    ss = R(k.ss[:, :], "ss")
    rstd = R(k.rstd[:, :], "rstd")
    junk = R(k.junk[s % 2][:, :], ("junk", 0))
    p.act(junk, src, AF.Square, accum=ss[:, s:s + 1])
    rsqrt_act(p, rstd[:, s:s + 1], ss[:, s:s + 1], R(k.eps1[:, :], "const"), 1.0 / D)
    if gR is not None:
        p.stt("dve", scratch, src, rstd[:, s:s + 1], gR, ALU.mult, ALU.mult)
    else:
        p.ts("dve", scratch, src, rstd[:, s:s + 1], None, ALU.mult)
    pt = ps_pair(k, pair)
    for c in range(8):
        p.tr(pt[:, c * 128:(c + 1) * 128], scratch[:, c * 128:(c + 1) * 128], R(k.identF[:, :], "const"))
    if gR is not None:
        p.copy("act", dstT, R(pt.ap.rearrange("p (c t) -> p c t", c=8), *pt.keys))
    else:
        for c in range(8):
            p.ts("dve" if c % 2 else "act_mul", dstT[:, c, :], pt[:, c * 128:(c + 1) * 128],
                 R(k.cm[:, gcm, c:c + 1], "const"), None, ALU.mult)


def rwkv_chunk(k, p, ns, hs, out_units, first_unit_of_core):
    di = k.di
    a = k.al
    m0 = a.mark()
    cst = k.cst
    cm = k.cm
    CK = lambda i, ct: R(cm[:, i, ct:ct + 1], "const")
    wts = {}
    for i, nm in enumerate(("wr", "wk", "wv")):
        wts[nm] = a.t(nm, [128, 8, D], BF16)
    wts["wo"] = a.t("wo", [128, 8, D], BF16)
    w1s = a.t("w1s", [128, 8, 64], BF16)
    a1s = a.t("a1s", [128, 8, 64], BF16)
    g1s = a.t("g1s", [128, 8, 128], BF16)
    w2s = a.t("w2s", [64, D], BF16)
    a2s = a.t("a2s", [64, D], BF16)
    g2s = a.t("g2s", [128, D], BF16)
    p.dma("pool", R(w1s[:, :, :], "w1s"), di["w1"].rearrange("(dc p) f -> p dc f", p=128), "rwl1")
    p.dma("pool", R(a1s[:, :, :], "a1s"), di["a1"].rearrange("(dc p) f -> p dc f", p=128), "rwl2")
    if out_units:
        p.dma("pool", R(g1s[:, :, :], "g1s"), di["g1"].rearrange("(dc p) f -> p dc f", p=128), "rwl3")
        p.dma("pool", R(g2s[:, :], "g2s"), di["g2"], "rwl6")
    for i, nm in ((2, "wv"), (1, "wk")):
        p.dma("pool", R(wts[nm][:, :, :], nm), di["w_rkv"][i].rearrange("(dc p) f -> p dc f", p=128), "rw" + nm)
    p.dma("pool", R(w2s[:, :], "w2s"), di["w2"], "rwl4")
    p.dma("pool", R(a2s[:, :], "a2s"), di["a2"], "rwl5")
    if out_units:
        p.dma("pool", R(wts["wr"][:, :, :], "wr"), di["w_rkv"][0].rearrange("(dc p) f -> p dc f", p=128), "rwwr")
        p.dma("pool", R(wts["wo"][:, :, :], "wo"), di["w_ro"].rearrange("(dc p) f -> p dc f", p=128), "rwwo")
    uT = a.t("uT", [128, 8, 257], BF16)
    m1 = a.mark()
    xx = a.t("xx", [128, 8, 256], BF16)
    xm = [a.t("xm", [128, 8, 256], BF16) for _ in range(2)]
    m2 = a.mark()
    a.reset(m1)
    gng = a.t("gng", [128, D], F32)
    gnb = a.t("gnb", [128, D], F32)
    g3bc = a.t("g3bc", [128, D], F32)
    assert a.mark() == m2
    R_gng = R(gng[:, :], "xx")
    R_gnb = R(gnb[:, :], *[("xm", 0, c) for c in range(8)])
    R_g3 = R(g3bc[:, :], *[("xm", 1, c) for c in range(8)])
    wl = a.t("wl", [64, 256], BF16)
    alr = a.t("alr", [64, 256], BF16)
    gls = a.t("gls", [128, 256], BF16)
    v_tm = a.t("v_tm", [128, 2, D], BF16)
    g_tm = a.t("g_tm", [128, 2, D], BF16)
    y_tm = a.t("y_tm", [128, 2, D], BF16)
    TN = ("r_f", "k_f", "sg", "a_f", "kk", "sq", "t1", "kmod", "b_f", "cs", "Enc")
    _m5 = a.mark()
    tf = {n: a.t(n, [128, 256], F32) for n in TN}
    _m6 = a.mark()
    a.reset(_m5)
    us2 = a.t("us2", [128, D], F32)
    a.reset(_m6)
    ALIAS = {"rn": "sq", "kkn": "kk", "Ec": "sq", "btf": "b_f", "ktf": "t1", "bhf": "kk", "khf": "k_f", "Enp": "sg"}
    T = lambda n: R(tf[ALIAS.get(n, n)][:, :], ALIAS.get(n, n))
    sqb = a.t("sqb", [128, 256], BF16)
    ARs = [a.t("AR", [128, 2, 256], BF16) for _ in range(2)]
    bts = [a.t("bt", [128, 256], BF16) for _ in range(2)]
    prb = sqb
    bkhq = [[a.t("bkhq", [128, 2, 2, 128], BF16) for _ in range(2)] for _ in range(3)]
    ARm = [a.t("ARm", [128, 2, 2, 256], BF16) for _ in range(3)]
    encE = [a.t("encE", [128, 4], F32) for _ in range(3)]
    btms = [a.t("btm", [128, 2, 256], BF16) for _ in range(2)]
    ktms = [a.t("ktm", [128, 2, 256], BF16) for _ in range(2)]
    G1e = [[a.t("G1e", [128, 2, 256], BF16) for _ in range(2)] for _ in range(2)]
    G2e = [[a.t("G2e", [128, 2, 256], BF16) for _ in range(2)] for _ in range(2)]
    G3e = [a.t("G3e", [128, 2, 128], BF16) for _ in range(2)]
    _m3 = a.mark()
    PQ = [[a.t("PQ", [128, 4, 128], BF16) for _ in range(2)] for _ in range(2)]
    _m4 = a.mark()
    a.reset(_m3)
    ygT = a.t("ygT", [128, 8, 128], BF16)
    ygT2 = a.t("ygT2", [128, 8, 128], BF16)
    assert a.mark() == _m4
    a.reset(_m4)
    _tm0 = [a.t("Tm0", [128, 2, 128], BF16) for _ in range(2)]
    Tm = [[[_tm0[s_], a.t("Tm1", [128, 2, 128], BF16)] for s_ in range(2)] for _ in range(2)]
    X1s = a.t("X1s", [128, 2, 64], BF16)
    Us = a.t("Us", [128, 2, 64], BF16)
    rks = a.t("rks", [128, 2, 16], F32)
    gst = {n: a.t(n, [128, 16], F32) for n in ("gs1", "gs2", "gmean", "gvar")}
    a.reset(m0)
    for ver in range(3):
        for q in range(2):
            p.memset("dve", R(bkhq[ver][q][:, :, :, :], ("bkhq", ver, q)), 0.0)
        p.memset("pool", R(ARm[ver][:, :, :, :], ("ARma", ver), ("ARmr", ver)), 0.0)
    for par in range(2):
        p.memset("pool", R(btms[par][:, :, :], ("btm", par)), 0.0)
        p.memset("pool", R(ktms[par][:, :, :], ("ktm", par)), 0.0)
    identF = R(k.identF[:, :], "const")
    ones64 = R(cst[:, 4, 0:64], "const")
    MS2 = R(cst[:, 0, :].rearrange("p (h n) -> p h n", h=2), "const")
    MST2 = R(cst[:, 1, 0:256].rearrange("p (h n) -> p h n", h=2), "const")
    blk1 = R(cst[:, 2, 0:128], "const")
    ind2 = R(cst[:, 2, 128:130], "const")
    ST = k.ST
    STb = k.STb
    nun = ns // 2
    for un in range(nun):
        need_out = un in out_units
        if first_unit_of_core and un == 0:
            p.memset("dve", R(k.ulast[:, :, :], "ulast"), 0.0)
        p.copy("dve", R(uT[:, :, 0:1], "uT0"), R(k.ulast[:, :, :], "ulast"))
        for s in range(2):
            norm_transpose(k, p, hs(un * 2 + s), None,
                           R(uT[:, :, 1 + s * 128:1 + (s + 1) * 128], ("uT", s)), s, R(k.junk[0][:, :], ("junk", 0)), 2 + s,
                           gcm=11 + 2)
        p.copy("dve", R(k.ulast[:, :, :], "ulast"), R(uT[:, :, 256:257], ("uT", 1)))
        R_uTc = lambda c: R(uT[:, c, 1:257], ("uT", 0), ("uT", 1))
        p.tt("dve", R(xx[:, :, :], "xx"), R(uT[:, :, 0:256], "uT0", ("uT", 0), ("uT", 1)),
             R(uT[:, :, 1:257], ("uT", 0), ("uT", 1)), ALU.subtract)

        def mix(m, buf):
            for c in range(8):
                p.stt("dve" if c % 2 == 0 else "pool", R(xm[buf][:, c, :], ("xm", buf, c)), R(xx[:, c, :], "xx"),
                      CK(m, c), R_uTc(c), ALU.mult, ALU.add)
            return lambda c, sl=slice(0, 256): R(xm[buf][:, c, sl], ("xm", buf, c))

        xw = mix(1, 0)
        pw = ps_bank(k, 0, 256)
        for c in range(8):
            p.mm(pw[0:64, :], R(w1s[:, c, :], "w1s"), xw(c), start=(c == 0), stop=(c == 7))
        wlf = R(tf["r_f"][0:64, :], "r_f")
        sigmoid_act(p, wlf, pw[0:64, :], R(k.zero1[0:64, :], "const"), R(k.one1[0:64, :], "const"), scale=2.0)
        p.act(R(wl[:, :], "wl"), wlf, AF.Identity, bias=R(k.mone1[0:64, :], "const"), scale=2.0)
        xa = mix(4, 1)
        pa = ps_bank(k, 1, 256)
        for c in range(8):
            p.mm(pa[0:64, :], R(a1s[:, c, :], "a1s"), xa(c), start=(c == 0), stop=(c == 7))
        p.copy("act", R(alr[:, :], "alr"), pa[0:64, :])
        if need_out:
            xg = mix(5, 0)
            pg = ps_bank(k, 0, 256)
            for c in range(8):
                p.mm(pg, R(g1s[:, c, :], "g1s"), xg(c), start=(c == 0), stop=(c == 7))
            glf = R(tf["r_f"][:, :], "r_f")
            sigmoid_act(p, glf, pg, R(k.zero1[:, :], "const"), R(k.one1[:, :], "const"))
            p.copy("act", R(gls[:, :], "gls"), glf)
            for s in range(2):
                po = ps_pair(k, 1)
                for hf in range(2):
                    p.mm(po[:, hf * 512:(hf + 1) * 512], R(gls[:, s * 128:(s + 1) * 128], "gls"),
                         R(g2s[:, hf * 512:(hf + 1) * 512], "g2s"))
                p.copy("act", R(g_tm[:, s, :], ("g_tm", s)), po)
        xv = mix(3, 1)
        for s in range(2):
            po = ps_pair(k, 2 + s)
            for hf in range(2):
                for c in range(8):
                    p.mm(po[:, hf * 512:(hf + 1) * 512], xv(c, slice(s * 128, (s + 1) * 128)),
                         R(wts["wv"][:, c, hf * 512:(hf + 1) * 512], "wv"), start=(c == 0), stop=(c == 7))
            p.copy("act" if s == 0 else "dve", R(v_tm[:, s, :], ("v_tm", s)), po)
        xr = mix(0, 0) if need_out else None
        xk = mix(2, 1)
        def P(ct):
            par = ct % 3
            p2 = ct % 2
            AR, bt, btm, ktm = ARs[p2], bts[p2], btms[p2], ktms[p2]
            kARa, kARr, kbt, kbtm, kktm = ("ARa", p2), ("ARr", p2), ("bt", p2), ("btm", p2), ("ktm", p2)
            b0 = ps_bank(k, 0)
            b1 = ps_bank(k, 1)
            if need_out:
                for c in range(8):
                    p.mm(b0[:, 0:256], R(wts["wr"][:, c, ct * 128:(ct + 1) * 128], "wr"), xr(c), start=(c == 0), stop=(c == 7))
            for c in range(8):
                p.mm(b0[:, 256:512], R(wts["wk"][:, c, ct * 128:(ct + 1) * 128], "wk"), xk(c), start=(c == 0), stop=(c == 7))
            p.mm(b1[:, 0:256], R(w2s[:, ct * 128:(ct + 1) * 128], "w2s"), R(wl[:, :], "wl"))
            p.mm(b1[:, 256:512], R(a2s[:, ct * 128:(ct + 1) * 128], "a2s"), R(alr[:, :], "alr"))
            yield
            if need_out:
                p.copy("act", T("r_f"), b0[:, 0:256])
            p.copy("dve", T("k_f"), b0[:, 256:512])
            sigmoid_act(p, T("sg"), b1[:, 0:256], R(k.ncm[:, 0, ct:ct + 1], "const"), R(k.one1[:, :], "const"))
            yield
            sigmoid_act(p, T("a_f"), b1[:, 256:512], R(k.ncm[:, 1, ct:ct + 1], "const"), R(k.one1[:, :], "const"))
            yield
            p.ts("dve", T("kk"), T("k_f"), CK(8, ct), None, ALU.mult)
            p.tt("dve", R(sqb[:, :], "sqb"), T("kk"), T("kk"), ALU.mult)
            yield
            b7 = ps_bank(k, 0, 256)
            p.mm(b7, blk1, R(sqb[:, :], "sqb"))
            rsqrt_act(p, T("rn"), b7, R(k.eps24[:, :], "const"), 1.0)
            yield
            p.ts("pool", T("t1"), T("a_f"), -1.0, CK(9, ct), ALU.add, ALU.mult)
            yield
            p.tt("dve", T("kkn"), T("kk"), T("rn"), ALU.mult)
            p.stt("pool", T("kmod"), T("t1"), 1.0, T("k_f"), ALU.add, ALU.mult)
            yield
            p.tt("dve", T("b_f"), T("kkn"), T("a_f"), ALU.mult)
            for q in range(4):
                sl = slice(q * 64, (q + 1) * 64)
                p.scan(T("cs")[:, sl], ones64, T("sg")[:, sl], 0.0, ALU.mult, ALU.add)
                if q % 2:
                    yield
            p.act(T("Ec"), T("cs"), AF.Exp, scale=C_DEC)
            p.act(T("Enc"), T("cs"), AF.Exp, scale=-C_DEC)
            p.tt("dve", T("t1"), T("cs"), T("sg"), ALU.subtract)
            yield
            p.act(T("Enp"), T("t1"), AF.Exp, scale=-C_DEC)
            if need_out:
                p.tt("dve", R(AR[:, 1, :], kARr), T("r_f"), T("Enc"), ALU.mult)
            yield
            p.copy("act", R(encE[par][:, :], ("encE", par)), T("Enc")[:, 63:256:64])
            p.stt("dve", R(AR[:, 0, :], kARa), T("kkn"), -1.0, T("Enp"), ALU.mult, ALU.mult)
            yield
            p.tt("dve", T("btf"), T("b_f"), T("Ec"), ALU.mult)
            p.tt("dve", T("ktf"), T("kmod"), T("Ec"), ALU.mult)
            yield
            p.copy("act", R(bt[:, :], kbt), T("btf"))
            for hh in range(2):
                rows = slice(hh * 64, hh * 64 + 64)
                p.copy("act", R(btm[rows, hh, :], kbtm), T("btf")[rows, :])
                p.copy("act", R(ktm[rows, hh, :], kktm), T("ktf")[rows, :])
                yield
                p.copy("act", R(ARm[par][rows, hh, 0, :], ("ARma", par)), R(AR[rows, 0, :], kARa))
                if need_out:
                    p.copy("act", R(ARm[par][rows, hh, 1, :], ("ARmr", par)), R(AR[rows, 1, :], kARr))
                yield
            encb = R(encE[par][:, :].unsqueeze(2).broadcast_to([128, 4, 64]), ("encE", par))
            v4 = lambda r_: R(r_.ap.rearrange("p (q n) -> p q n", q=4), *r_.keys)
            p.tt("dve", v4(T("bhf")), v4(T("btf")), encb, ALU.mult)
            yield
            p.tt("pool", v4(T("khf")), v4(T("ktf")), encb, ALU.mult)
            yield
            b7f = ps_bank(k, 1)
            for ai, nm in enumerate(("bhf", "khf")):
                for s in range(2):
                    o = (ai * 2 + s) * 128
                    p.tr(b7f[:, o:o + 128], T(nm)[:, s * 128:(s + 1) * 128], identF)
            yield
            b7v = b7f.ap.rearrange("p (a s c) -> p a s c", a=2, s=2)
            p.copy("act", R(bkhq[par][0][0:64, :, :, :], ("bkhq", par, 0)), R(b7v[0:64], *b7f.keys))
            p.copy("act", R(bkhq[par][1][64:128, :, :, :], ("bkhq", par, 1)), R(b7v[64:128], *b7f.keys))
            yield
            if need_out:
                p.stt("dve", R(prb[:, :], "sqb"), T("r_f"), CK(10, ct), T("kmod"), ALU.mult, ALU.mult)
                for s in range(2):
                    p.mm(R(k.ps[3][:, 512 + s * 16 + 2 * ct: 512 + s * 16 + 2 * ct + 2], ("ps", 7)),
                         R(prb[:, s * 128:(s + 1) * 128], "sqb"), ind2)
            yield

        def I(ct, s):
            par = ct % 3
            p2 = ct % 2
            AR, bt, btm, ktm = ARs[p2], bts[p2], btms[p2], ktms[p2]
            kARa, kARr, kbt, kbtm, kktm = ("ARa", p2), ("ARr", p2), ("bt", p2), ("btm", p2), ("ktm", p2)
            tok = slice(s * 128, (s + 1) * 128)
            bankA = ps_bank(k, 2 + s)
            bankB = ps_bank(k, 4 + s)
            g1, g2, g3 = G1e[p2][s], G2e[p2][s], G3e[s]
            kG1, kG2, kG3 = ("G1e", p2, s), ("G2e", p2, s), ("G3e", s)
            nw = 256 if need_out else 128
            rhs = R(AR[:, :, tok], kARa, kARr) if need_out else R(AR[:, 0, tok], kARa)
            for hh in range(2):
                p.mm(bankA[:, hh * 256:hh * 256 + nw], R(btm[:, hh, tok], kbtm), rhs)
            yield
            p.tt("dve", R(g1[:, :, 0:nw], kG1), R(bankA.ap.rearrange("p (h n) -> p h n", h=2)[:, :, 0:nw], *bankA.keys), MS2[:, :, 0:nw], ALU.mult)
            for hh in range(2):
                p.mm(bankB[:, hh * 128:(hh + 1) * 128], R(ARm[par][:, hh, 0, tok], ("ARma", par)), R(bt[:, tok], kbt))
            yield
            p.tt("dve", R(g3[:, :, :], kG3), R(bankB.ap[:, 0:256].rearrange("p (h n) -> p h n", h=2), *bankB.keys), MST2, ALU.mult)
            for hh in range(2):
                p.mm(bankA[:, hh * 256:hh * 256 + nw], R(ktm[:, hh, tok], kktm), rhs)
            yield
            p.tt("dve", R(g2[:, :, 0:nw], kG2), R(bankA.ap.rearrange("p (h n) -> p h n", h=2)[:, :, 0:nw], *bankA.keys), MS2[:, :, 0:nw], ALU.mult)
            Aab = lambda hh: R(g1[:, hh, 0:128], kG1)
            tm = Tm[p2][s]
            tkey = lambda r_: ("Tm0", s) if r_ == 0 else ("Tm", p2, s, 1)
            for hh in range(2):
                p.tt("pool", R(tm[0][:, hh, :], ("Tm0", s)), Aab(hh), identF, ALU.add)
            yield
            Pp = [Aab(0), Aab(1)]
            Qp = [R(g3[:, 0, :], kG3), R(g3[:, 1, :], kG3)]
            tcur = 0
            for lvl in range(1, 6):
                pq = PQ[s][lvl % 2]
                kpq = ("PQ", s, lvl % 2)
                pT = bankA[:, 0:256]
                for hh in range(2):
                    if lvl <= 4:
                        p.mm(bankB[:, hh * 128:(hh + 1) * 128], Qp[hh], Pp[hh])
                    p.mm(bankB[:, (2 + hh) * 128:(3 + hh) * 128], Pp[hh], Qp[hh])
                yield
                lo = 0 if lvl <= 4 else 2
                p.copy("act", R(pq[:, lo:4, :], kpq),
                       R(bankB.ap.rearrange("p (a n) -> p a n", a=4)[:, lo:4, :], *bankB.keys))
                yield
                Pp = [R(pq[:, 0, :], kpq), R(pq[:, 1, :], kpq)]
                Qp = [R(pq[:, 2, :], kpq), R(pq[:, 3, :], kpq)]
                for hh in range(2):
                    p.mm(pT[:, hh * 128:(hh + 1) * 128], Qp[hh], R(tm[tcur][:, hh, :], tkey(tcur)))
                yield
                p.tt("dve", R(tm[1 - tcur][:, :, :], tkey(1 - tcur)),
                     R(pT.ap.rearrange("p (h n) -> p h n", h=2), *pT.keys), R(tm[tcur][:, :, :], tkey(tcur)), ALU.add)
                tcur = 1 - tcur
                yield
            assert tcur == 1

        def C(ct):
            par = ct % 3
            p2 = ct % 2
            for s in range(2):
                tok = slice(s * 128, (s + 1) * 128)
                g1, g2 = G1e[p2][s], G2e[p2][s]
                kG1, kG2 = ("G1e", p2, s), ("G2e", p2, s)
                Arb = lambda hh: R(g1[:, hh, 128:256], kG1)
                Aak = lambda hh: R(g2[:, hh, 0:128], kG2)
                Ark = lambda hh: R(g2[:, hh, 128:256], kG2)
                Tf = lambda hh: R(Tm[p2][s][1][:, hh, :], ("Tm", p2, s, 1))
                for q in range(2):
                    kb = slice(q * 64, q * 64 + 64)
                    pX = R(k.ps[3][:, 0:128].rearrange("p (h n) -> p h n", h=2), ("ps", 6))
                    pU = R(k.ps[3][:, 128:256].rearrange("p (h n) -> p h n", h=2), ("ps", 6))
                    pY = R(k.ps[3][:, 256:384].rearrange("p (h n) -> p h n", h=2), ("ps", 6))
                    pS = R(k.ps[3][:, 384:512].rearrange("p (h n) -> p h n", h=2), ("ps", 6))
                    for hh in range(2):
                        hc = slice((2 * ct + hh) * 64, (2 * ct + hh + 1) * 64)
                        p.mm(pX[:, hh, :], R(ARm[par][:, hh, 0, tok], ("ARma", par)), R(STb[:, ct, :], ("STb", ct)), start=True, stop=False)
                        p.mm(pX[:, hh, :], Aak(hh), R(v_tm[:, s, hc], ("v_tm", s)), start=False, stop=True)
                    yield
                    p.copy("act", R(X1s[:, :, :], "X1s"), pX)
                    yield
                    for hh in range(2):
                        p.mm(pU[:, hh, :], Tf(hh), R(X1s[:, hh, :], "X1s"))
                    yield
                    p.copy("dve", R(Us[:, :, :], "Us"), pU)
                    yield
                    if need_out:
                        for hh in range(2):
                            hc = slice((2 * ct + hh) * 64, (2 * ct + hh + 1) * 64)
                            p.mm(pY[:, hh, :], R(ARm[par][:, hh, 1, tok], ("ARmr", par)), R(STb[:, ct, :], ("STb", ct)), start=True, stop=False)
                            p.mm(pY[:, hh, :], Arb(hh), R(Us[:, hh, :], "Us"), start=False, stop=False)
                            p.mm(pY[:, hh, :], Ark(hh), R(v_tm[:, s, hc], ("v_tm", s)), start=False, stop=True)
                    for hh in range(2):
                        hc = slice((2 * ct + hh) * 64, (2 * ct + hh + 1) * 64)
                        p.mm(pS[:, hh, :], R(bkhq[par][q][:, 0, s, :], ("bkhq", par, q)), R(Us[:, hh, :], "Us"), start=True, stop=False)
                        p.mm(pS[:, hh, :], R(bkhq[par][q][:, 1, s, :], ("bkhq", par, q)), R(v_tm[:, s, hc], ("v_tm", s)), start=False, stop=True)
                    yield
                    if need_out:
                        p.copy("act", R(y_tm[kb, s, 2 * ct * 64:(2 * ct + 2) * 64].rearrange("p (h n) -> p h n", h=2), ("y_tm", s)),
                               pY[kb, :, :])
                    ce = s * 2 + q
                    for hh in range(2):
                        pb = slice(hh * 64, hh * 64 + 64)
                        p.stt("dve", R(STb[pb, ct, :], ("STb", ct)), R(ST[pb, ct, :], ("ST", ct)),
                              R(encE[par][pb, ce:ce + 1], ("encE", par)), pS[pb, hh, :], ALU.mult, ALU.add)
                    yield
                    for hh in range(2):
                        pb = slice(hh * 64, hh * 64 + 64)
                        p.stt("dve", R(ST[pb, ct, :], ("ST", ct)), R(ST[pb, ct, :], ("ST", ct)),
                              R(encE[par][pb, ce:ce + 1], ("encE", par)), pS[pb, hh, :], ALU.mult, ALU.add)
                    yield

        def interleave(*gens):
            gens = [g for g in gens if g is not None]
            while gens:
                for g in list(gens):
                    try:
                        next(g)
                    except StopIteration:
                        gens.remove(g)

        interleave(P(0))
        for kk_ in range(9):
            interleave(I(kk_, 0) if kk_ < 8 else None, I(kk_, 1) if kk_ < 8 else None,
                       C(kk_ - 1) if kk_ >= 1 else None, P(kk_ + 1) if kk_ + 1 < 8 else None)
        if not need_out:
            continue
        p.copy("dve", R(rks[:, :, :], "rks"), R(k.ps[3][:, 512:544].rearrange("p (s h) -> p s h", s=2), ("ps", 7)))
        k.ngl = getattr(k, "ngl", 0) + 1
        p.dma("sp", R_gng, di["tmb"][0], f"gl{k.ngl % 2}a")
        p.dma("sp", R_gnb, di["tmb"][1], f"gl{k.ngl % 2}b")
        p.dma("sp", R_g3, di["gbc"][3], f"gl{k.ngl % 2}c")
        def OUT(s):
            ys = R(y_tm[:, s, :], ("y_tm", s))
            y3 = R(y_tm[:, s, :].rearrange("p (h n) -> p h n", h=16), ("y_tm", s))
            if s == 0:
                us_t, ukeys = k.junk[0], (("junk", 0),)
                yT, ykeys = ygT, (("PQ", 0, 0), ("PQ", 0, 1))
                gt = [gst[n][:, :] for n in ("gs1", "gs2", "gmean", "gvar")]
                gkeys = ["gs1", "gs2", "gmean", "gvar"]
            else:
                us_t, ukeys = us2, ("r_f", "k_f", "sg", "a_f")
                yT, ykeys = ygT2, (("PQ", 1, 0), ("PQ", 1, 1))
                gt = [tf["kk"][:, i * 16:(i + 1) * 16] for i in range(4)]
                gkeys = ["kk"] * 4
            us = R(us_t[:, :], *ukeys)
            us3 = R(us_t[:, :].rearrange("p (h n) -> p h n", h=16), *ukeys)
            gs1, gs2, gmean, gvar = [R(gt[i], gkeys[i]) for i in range(4)]
            bc = lambda r_: R(r_.ap.unsqueeze(2).broadcast_to([128, 16, 64]), *r_.keys)
            p.reduce("dve", gs1, y3, ALU.add)
            p.act(us, ys, AF.Square)
            yield
            p.reduce("dve", gs2, us3, ALU.add)
            p.ts("dve", gmean, gs1, 1.0 / 64, None, ALU.mult)
            yield
            p.tt("dve", gvar, gmean, gmean, ALU.mult)
            p.stt("dve", gvar, gs2, 1.0 / 64, gvar, ALU.mult, ALU.subtract)
            yield
            rsqrt_act(p, gvar, gvar, R(k.epsgn[:, :], "const"), 1.0)
            p.tt("dve", us3, y3, bc(gmean), ALU.subtract)
            yield
            p.tt("dve", us3, us3, bc(gvar), ALU.mult)
            v3 = R(v_tm[:, s, :].rearrange("p (h n) -> p h n", h=16), ("v_tm", s))
            p.tt("pool", y3, v3, bc(R(rks[:, s, :], "rks")), ALU.mult)
            yield
            p.tt("dve", us, us, R_gng, ALU.mult)
            yield
            p.tt("pool", us, us, R_gnb, ALU.add)
            yield
            p.tt("dve", us, us, ys, ALU.add)
            yield
            p.tt("dve", us, us, R(g_tm[:, s, :], ("g_tm", s)), ALU.mult)
            yield
            pt = ps_pair(k, 2 + s)
            for c in range(8):
                p.tr(pt[:, c * 128:(c + 1) * 128], us[:, c * 128:(c + 1) * 128], identF)
            yield
            p.copy("act", R(yT[:, :, :], *ykeys), R(pt.ap.rearrange("p (c t) -> p c t", c=8), *pt.keys))
            yield
            po = ps_pair(k, s)
            for hf in range(2):
                for c in range(8):
                    p.mm(po[:, hf * 512:(hf + 1) * 512], R(yT[:, c, :], *ykeys),
                         R(wts["wo"][:, c, hf * 512:(hf + 1) * 512], "wo"), start=(c == 0), stop=(c == 7))
            yield
            post_norm_residual(k, p, po, hs(un * 2 + s), R_g3, s, False, junk=us)
            yield

        interleave(OUT(0), OUT(1))


def kv_proj(k, p, slots, hs, shift):
    di = k.di
    a = k.al
    m0 = a.mark()
    wkv = a.t("wkv", [128, 8, 512], BF16)
    hnT = a.t("hnT", [128, 8, 128], BF16)
    usc = a.t("usc2", [128, D], F32)
    bvb = a.t("bvb", [128, 256], F32)
    a.reset(m0)
    p.dma("pool", R(wkv[:, :, :], "wkv"), di["w_kv"].rearrange("(dc p) f -> p dc f", p=128), "wkv")
    p.dma("sp", R(bvb[:, :], "bvb"), di["tmb"][3][:, 0:256], "par")
    if shift:
        n8 = k.ns_kv
        p.copy("dve", R(k.KTm[:, :, 0:128], ("kt", 0)), R(k.KTm[:, :, n8 * 128:(n8 + 1) * 128], ("kt", n8)))
        p.copy("pool", R(k.VA[:, 0, :, :], ("va", 0)), R(k.VA[:, n8, :, :], ("va", n8)))
    for j in slots:
        norm_transpose(k, p, hs(j), None, R(hnT[:, :, :], "hnT"), j % 2, R(usc[:, :], "usc2"), 2 + (j % 2), gcm=14)
        pk = ps_bank(k, 0, 256)
        for kt in range(2):
            for c in range(8):
                p.mm(pk[:, kt * 128:(kt + 1) * 128], R(wkv[:, c, kt * 128:(kt + 1) * 128], "wkv"),
                     R(hnT[:, c, :], "hnT"), start=(c == 0), stop=(c == 7))
        for kt in range(2):
            p.act(R(k.KTm[:, kt, (j + 1) * 128:(j + 2) * 128], ("kt", j + 1)),
                  pk[:, kt * 128:(kt + 1) * 128], AF.Identity, bias=R(k.cm[:, 12, kt:kt + 1], "const"))
        pv = ps_bank(k, 1, 256)
        for c in range(8):
            p.mm(pv, R(hnT[:, c, :], "hnT"), R(wkv[:, c, 256:512], "wkv"), start=(c == 0), stop=(c == 7))
        p.tt("dve", R(k.VA[:, j + 1, :, 0:64], ("va", j + 1)),
             R(pv.ap.rearrange("p (g n) -> p g n", g=4), *pv.keys),
             R(bvb[:, :].rearrange("p (g n) -> p g n", g=4), "bvb"), ALU.add)


def attention(k, p, nblk, hs, first_chunk_of_own):
    di = k.di
    a = k.al
    m0 = a.mark()
    wq = a.t("wq", [128, 8, D], BF16)
    wao = a.t("wao", [128, 8, D], BF16)
    uT1 = [a.t("uT1", [128, 8, 128], BF16) for _ in range(2)]
    qT = [a.t("qT", [128, 2, 8, 128], BF16) for _ in range(2)]
    usc = [a.t("usc3", [128, D], F32) for _ in range(2)]
    pex = [[a.t("pex", [128, 2, 512], BF16) for _ in range(2)] for _ in range(2)]
    o_tm = [a.t("o_tm", [128, D], F32) for _ in range(2)]
    oT = [a.t("oT", [128, 8, 128], BF16) for _ in range(2)]
    msb = [a.t("msb", [128, D], F32) for _ in range(2)]
    den = [a.t("den", [128, 4], F32) for _ in range(2)]
    bob = a.t("bob", [128, D], F32)
    g3b = a.t("g3b", [128, D], F32)
    esk = a.t("esk", [128, 16], F32)
    mfs = a.t("mfs", [128, 128], BF16)
    a.reset(m0)
    p.dma("pool", R(wq[:, :, :], "wq"), di["w_q"].rearrange("(dc p) f -> p dc f", p=128), "wq")
    p.dma("pool", R(wao[:, :, :], "wao"), di["w_ao"].rearrange("(dc p) f -> p dc f", p=128), "wao")
    p.dma("pool", R(mfs[:, :], "mfs"), di["mfirst"], "mfs")
    p.dma("sp", R(bob[:, :], "bob"), di["tmb"][2], "par")
    p.dma("sp", R(g3b[:, :], "g3b"), di["gbc"][9], "par")
    p.dma("sp", R(esk[:, :], "esk"), di["tmb"][3][:, 256:272], "par")
    p.act(R(esk[:, :], "esk"), R(esk[:, :], "esk"), AF.Exp)
    for par in range(2):
        p.memset("dve", R(qT[par][:, :, :, :], ("qT", par)), 0.0)
    identF = R(k.identF[:, :], "const")
    m_cur = R(k.cst[:, 3, 0:128].unsqueeze(1).broadcast_to([128, 4, 128]), "const")
    m_prev = R(k.cst[:, 3, 128:256].unsqueeze(1).broadcast_to([128, 4, 128]), "const")
    m_first = R(mfs[:, :].unsqueeze(1).broadcast_to([128, 4, 128]), "mfs")

    def block(b):
        par = b % 2
        kq, ku, ko, kot, kms, kdn = ("qT", par), ("uT1", par), ("o_tm", par), ("oT", par), ("msb", par), ("den", par)
        norm_transpose(k, p, hs(b), None, R(uT1[par][:, :, :], ku), b % 2, R(usc[par][:, :], ("usc3", par)), 2 + (b % 2), gcm=15)
        yield
        for t in range(8):
            pq = ps_bank(k, par, 128)
            for c in range(8):
                p.mm(pq, R(wq[:, c, t * 128:(t + 1) * 128], "wq"), R(uT1[par][:, c, :], ku), start=(c == 0), stop=(c == 7))
            for half in range(2):
                rows = slice(half * 64, half * 64 + 64)
                p.act(R(qT[par][rows, half, t, :], kq), pq[rows, :], AF.Identity, bias=R(k.cm[rows, 11, t:t + 1], "const"))
            yield
        for g in range(4):
            pp, half = g // 2, g % 2
            px = pex[par][g % 2]
            for kb, slot in ((0, b), (1, b + 1)):
                ps = ps_bank(k, 2 + par)
                p.mm(ps, R(k.KTm[:, pp, slot * 128:(slot + 1) * 128], ("kt", slot)),
                     R(qT[par][:, half, pp * 4:pp * 4 + 4, :], kq))
                pxr = R(px[:, kb, :], ("pex", par, g % 2, kb))
                p.act(pxr, ps, AF.Exp, scale=0.125)
                if kb == 1:
                    mk = m_cur
                elif b == 0 and first_chunk_of_own:
                    mk = m_first
                else:
                    mk = m_prev
                pxv = R(px[:, kb, :].rearrange("p (i q) -> p i q", i=4), ("pex", par, g % 2, kb))
                p.tt("dve" if kb == 0 else "pool", pxv, pxv, mk, ALU.mult)
                yield
            po = ps_bank(k, 4 + 2 * par + (g % 2), 260)
            for i in range(4):
                for kb, slot in ((0, b), (1, b + 1)):
                    p.mm(po[:, i * 65:(i + 1) * 65], R(px[:, kb, i * 128:(i + 1) * 128], ("pex", par, g % 2, kb)),
                         R(k.VA[:, slot, g, :], ("va", slot)), start=(kb == 0), stop=(kb == 1))
            yield
            po3 = R(po.ap.rearrange("p (i n) -> p i n", i=4), *po.keys)
            dn = R(den[par][:, :], kdn)
            p.tt("dve", dn, po3[:, :, 64], R(esk[:, 4 * g:4 * g + 4], "esk"), ALU.add)
            p.recip(dn, dn)
            p.tt("dve", R(o_tm[par][:, g * 256:(g + 1) * 256].rearrange("p (i n) -> p i n", i=4), ko),
                 po3[:, :, 0:64], R(den[par][:, :].unsqueeze(2).broadcast_to([128, 4, 64]), kdn), ALU.mult)
            yield
        pt = ps_pair(k, 2 + par)
        for c in range(8):
            p.tr(pt[:, c * 128:(c + 1) * 128], R(o_tm[par][:, c * 128:(c + 1) * 128], ko), identF)
        yield
        p.copy("act", R(oT[par][:, :, :], kot), R(pt.ap.rearrange("p (c t) -> p c t", c=8), *pt.keys))
        yield
        pm = ps_pair(k, 2 + par)
        for hf in range(2):
            for c in range(8):
                p.mm(pm[:, hf * 512:(hf + 1) * 512], R(oT[par][:, c, :], kot),
                     R(wao[:, c, hf * 512:(hf + 1) * 512], "wao"), start=(c == 0), stop=(c == 7))
        yield
        ms = R(msb[par][:, :], kms)
        p.tt("dve", ms, pm, R(bob[:, :], "bob"), ALU.add)
        yield
        post_norm_residual(k, p, ms, hs(b), R(g3b[:, :], "g3b"), b, False)
        yield

    def interleave(*gens):
        gens = list(gens)
        while gens:
            for g_ in list(gens):
                try:
                    next(g_)
                except StopIteration:
                    gens.remove(g_)

    for b in range(0, nblk, 2):
        if b + 1 < nblk:
            interleave(block(b), block(b + 1))
        else:
            interleave(block(b))


def full_cfg():
    chunks = [
        dict(sub0=0, nsub=8, out0=0, out_units=[], kv_slots=[], kv_shift=False, own=False, first_own=False),
        dict(sub0=8, nsub=8, out0=0, out_units=[3], kv_slots=[7], kv_shift=False, own=False, first_own=False),
        dict(sub0=16, nsub=8, out0=0, out_units=[0, 1, 2, 3], kv_slots=list(range(8)), kv_shift=True, own=True, first_own=True),
        dict(sub0=24, nsub=8, out0=8, out_units=[0, 1, 2, 3], kv_slots=list(range(8)), kv_shift=True, own=True, first_own=False),
    ]
    return dict(chunks=chunks, ntw=4096, nout=2048, stages="full")


def kernel(**inputs):
    g = host_prep(inputs)
    x = np.asarray(inputs["x"], np.float32)
    B, T, _ = x.shape
    ii = np.arange(128)
    mprev = (ii[:, None] > ii[None, :]).astype(np.float32)
    in_maps = []
    for core in range(8):
        b, half = core // 2, core % 2
        m = dict(g)
        if half == 0:
            xw = np.concatenate([np.zeros((2048, D), np.float32), x[b, 0:2048]], 0)
            m["mfirst"] = np.zeros((128, 128), np.float32)
        else:
            xw = x[b]
            m["mfirst"] = mprev
        m["xw"] = np.ascontiguousarray(xw)
        in_maps.append(m)
    nc, p = build(full_cfg())
    res = run_bass_kernel_spmd(nc, in_maps, core_ids=list(range(8)))
    out = np.zeros((B, T, D), np.float32)
    for core in range(8):
        b, half = core // 2, core % 2
        out[b, half * 2048:(half + 1) * 2048] = res.results[core]["out"]
    return out
```
